# Optimizing a Trainium2 kernel written in Bass

```python
import math
import jax
import jax.numpy as jnp
from jax import lax
import numpy as np

D_MODEL = 2048
BATCH = 4
SEQ = 4096
DEPTH = 2

N_EVEN = (DEPTH + 1) // 2
N_ODD = DEPTH // 2
MEM_LEN = 256
EPS = 1e-6
Q_BLOCK = 128

GDN_HEADS = 8
GDN_DK = 128
GDN_DV = 128
GDN_CONV = 5
GDN_CHUNK = 64
GDN_CONV_CH = GDN_HEADS * (2 * GDN_DK + GDN_DV)

DIFF_HEADS = 8
DIFF_DQK = 64
DIFF_DV = 128

SWA_HEADS = 8
SWA_KV_HEADS = 2
SWA_DH = 128
WINDOW = 128

MLA_HEADS = 8
MLA_NOPE = 128
MLA_ROPE = 64
MLA_DV = 128
MLA_KV_RANK = 512
ROPE_BASE = 10000.0

MEM_HEADS = 4
MEM_DH = 128

REL_BUCKETS = 32
REL_MAX_DIST = 128
REL_HEADS = 8

EV_SPLITS = [GDN_HEADS * GDN_DK, GDN_HEADS * GDN_DK, GDN_HEADS * GDN_DV, 2 * GDN_HEADS, 2 * GDN_HEADS, GDN_HEADS * GDN_DV,
             DIFF_HEADS * 2 * DIFF_DQK, DIFF_HEADS * 2 * DIFF_DQK, DIFF_HEADS * DIFF_DV, DIFF_HEADS * DIFF_DV,
             MEM_HEADS * MEM_DH, MEM_HEADS * MEM_DH]
EV_IN = sum(EV_SPLITS)
EV_OUT = GDN_HEADS * GDN_DV + DIFF_HEADS * DIFF_DV + MEM_HEADS * MEM_DH
OD_SPLITS = [SWA_HEADS * SWA_DH, SWA_KV_HEADS * SWA_DH, SWA_KV_HEADS * SWA_DH, SWA_HEADS * SWA_DH,
             MLA_HEADS * (MLA_NOPE + MLA_ROPE), MLA_KV_RANK, MLA_ROPE, MLA_HEADS * MLA_DV,
             MEM_HEADS * MEM_DH, MEM_HEADS * MEM_DH]
OD_IN = sum(OD_SPLITS)
OD_OUT = SWA_HEADS * SWA_DH + MLA_HEADS * MLA_DV + MEM_HEADS * MEM_DH

kernel_name = 'hybrid_bidir_gdn_diff_swa_mla_mem'

F32 = jnp.float32


def rms_norm(x, gain):
    xf = x.astype(F32)
    y = xf * lax.rsqrt(jnp.mean(xf * xf, axis=-1, keepdims=True) + EPS)
    return (y * gain.astype(F32)).astype(x.dtype)


def l2_norm(x):
    xf = x.astype(F32)
    return (xf * lax.rsqrt(jnp.sum(xf * xf, axis=-1, keepdims=True) + EPS)).astype(x.dtype)


def split_cols(h, sizes):
    return jnp.split(h, np.cumsum(sizes)[:-1].tolist(), axis=-1)


def t5_bucket(rel):
    half = REL_BUCKETS // 2
    max_exact = half // 2
    ret = (rel > 0).astype(jnp.int32) * half
    n = jnp.abs(rel)
    nf = jnp.maximum(n, 1).astype(F32)
    large = max_exact + (jnp.log(nf / max_exact) / math.log(REL_MAX_DIST / max_exact) * (half - max_exact)).astype(jnp.int32)
    large = jnp.minimum(large, half - 1)
    return ret + jnp.where(n < max_exact, n, large)


def rope(t, pos):
    half = t.shape[-1] // 2
    inv_freq = ROPE_BASE ** (-jnp.arange(half, dtype=F32) / half)
    ang = pos.astype(F32)[:, :, None, None] * inv_freq
    cos, sin = jnp.cos(ang), jnp.sin(ang)
    t1, t2 = t[..., :half].astype(F32), t[..., half:].astype(F32)
    return jnp.concatenate([t1 * cos - t2 * sin, t1 * sin + t2 * cos], axis=-1).astype(t.dtype)


def short_conv(x, w):
    pad = (GDN_CONV - 1) // 2
    return lax.conv_general_dilated(x, w[:, None, :].astype(x.dtype), window_strides=(1,), padding=[(pad, pad)],
                                    dimension_numbers=('NWC', 'WIO', 'NWC'), feature_group_count=x.shape[-1])


def gated_delta_chunked(q, k, v, g, beta):
    bsz, nh, seq, dk = q.shape
    dv = v.shape[-1]
    c = GDN_CHUNK
    nc = seq // c
    q = q * (dk ** -0.5)
    rs = lambda t: t.reshape(bsz, nh, nc, c, *t.shape[3:])
    q, k, v, g, beta = rs(q), rs(k), rs(v), rs(g), rs(beta)
    g = jnp.cumsum(g, axis=-1)
    lower_incl = jnp.tril(jnp.ones((c, c), bool))
    strict = jnp.tril(jnp.ones((c, c), bool), -1)
    diff = g[..., :, None] - g[..., None, :]
    decay = jnp.where(lower_incl, jnp.exp(jnp.where(lower_incl, diff, 0.0)), 0.0)
    k_beta = k * beta[..., None]
    m = jnp.where(strict, jnp.einsum('bhncd,bhnsd->bhncs', k_beta, k) * decay, 0.0)
    eye = jnp.eye(c, dtype=q.dtype)
    t_inv = lax.linalg.triangular_solve(eye + m, jnp.broadcast_to(eye, m.shape), left_side=True, lower=True,
                                        unit_diagonal=True)
    u = t_inv @ (v * beta[..., None])
    w = t_inv @ (k_beta * jnp.exp(g)[..., None])
    a_intra = jnp.where(lower_incl, jnp.einsum('bhncd,bhnsd->bhncs', q, k) * decay, 0.0)
    g_last = g[..., -1]
    q_dec = q * jnp.exp(g)[..., None]
    k_dec = k * jnp.exp(g_last[..., None] - g)[..., None]

    def step(state, inp):
        q_d, k_d, u_c, w_c, a_c, gl = inp
        v_new = u_c - jnp.einsum('bhcd,bhde->bhce', w_c, state)
        o = jnp.einsum('bhcd,bhde->bhce', q_d, state) + jnp.einsum('bhcs,bhse->bhce', a_c, v_new)
        state = state * jnp.exp(gl)[..., None, None] + jnp.einsum('bhcd,bhce->bhde', k_d, v_new)
        return state, o

    xs = tuple(jnp.moveaxis(t, 2, 0) for t in (q_dec, k_dec, u, w, a_intra, g_last))
    s0 = jnp.zeros((bsz, nh, dk, dv), q.dtype)
    _, o = lax.scan(step, s0, xs)
    return jnp.moveaxis(o, 0, 2).reshape(bsz, nh, seq, dv)


def gdn_branch(q, k, v, b, a, gate, conv_w, a_log, dt_bias, out_gain):
    bsz, seq, _ = q.shape
    qkv = jax.nn.silu(short_conv(jnp.concatenate([q, k, v], axis=-1), conv_w))
    q, k, v = split_cols(qkv, [GDN_HEADS * GDN_DK, GDN_HEADS * GDN_DK, GDN_HEADS * GDN_DV])

    def heads(t, d):
        return t.reshape(bsz, seq, GDN_HEADS, d).transpose(0, 2, 1, 3).astype(F32)

    q = l2_norm(heads(q, GDN_DK))
    k = l2_norm(heads(k, GDN_DK))
    v = heads(v, GDN_DV)

    def per_dir(t):
        return t.astype(F32).reshape(bsz, seq, 2, GDN_HEADS).transpose(2, 0, 3, 1)

    beta = jax.nn.sigmoid(per_dir(b))
    g = -jnp.exp(a_log.astype(F32))[:, None, :, None] * jax.nn.softplus(per_dir(a) + dt_bias.astype(F32)[:, None, :, None])
    o_fwd = gated_delta_chunked(q, k, v, g[0], beta[0])
    rev = lambda t: jnp.flip(t, axis=2)
    o_bwd = rev(gated_delta_chunked(rev(q), rev(k), rev(v), rev(g[1]), rev(beta[1])))
    o = (o_fwd + o_bwd).transpose(0, 2, 1, 3)
    o = rms_norm(o, out_gain) * jax.nn.silu(gate.reshape(bsz, seq, GDN_HEADS, GDN_DV).astype(F32))
    return o.reshape(bsz, seq, GDN_HEADS * GDN_DV).astype(gate.dtype)


def diff_attention(q, k, v, gate, rel_table, q_gain, k_gain, lam, subln_gain, lambda_init):
    bsz, seq, _ = q.shape
    nb = seq // Q_BLOCK
    q = rms_norm(q.reshape(bsz, seq, DIFF_HEADS, 2, DIFF_DQK), q_gain)
    k = rms_norm(k.reshape(bsz, seq, DIFF_HEADS, 2, DIFF_DQK), k_gain)
    v = v.reshape(bsz, seq, DIFF_HEADS, DIFF_DV)
    lam = lam.astype(F32)
    lam_full = jnp.exp(jnp.sum(lam[0] * lam[1])) - jnp.exp(jnp.sum(lam[2] * lam[3])) + lambda_init
    scale = DIFF_DQK ** -0.5
    key_idx = jnp.arange(seq)
    qb = q.reshape(bsz, nb, Q_BLOCK, DIFF_HEADS, 2, DIFF_DQK).swapaxes(0, 1)

    def block(args):
        qblk, n = args
        q_idx = n * Q_BLOCK + jnp.arange(Q_BLOCK)
        bias = rel_table[t5_bucket(key_idx[None, :] - q_idx[:, None])].transpose(2, 0, 1).astype(F32)
        s = jnp.einsum('bqhcd,bkhcd->bchqk', qblk, k).astype(F32) * scale + bias[None, None]
        p = jax.nn.softmax(s, axis=-1)
        p = p[:, 0] - lam_full * p[:, 1]
        return jnp.einsum('bhqk,bkhd->bqhd', p.astype(v.dtype), v)

    o = lax.map(block, (qb, jnp.arange(nb))).swapaxes(0, 1).reshape(bsz, seq, DIFF_HEADS, DIFF_DV)
    o = rms_norm(o, subln_gain) * (1.0 - lambda_init)
    o = o * jax.nn.silu(gate.reshape(bsz, seq, DIFF_HEADS, DIFF_DV))
    return o.reshape(bsz, seq, DIFF_HEADS * DIFF_DV)


def window_attention(q, k, v, gate, rel_table, q_gain, k_gain, sink):
    bsz, seq, _ = q.shape
    grp = SWA_HEADS // SWA_KV_HEADS
    nb = seq // Q_BLOCK
    nbr = WINDOW // Q_BLOCK
    span = 2 * nbr + 1
    q = rms_norm(q.reshape(bsz, nb, Q_BLOCK, SWA_KV_HEADS, grp, SWA_DH), q_gain)
    k = rms_norm(k.reshape(bsz, seq, SWA_KV_HEADS, SWA_DH), k_gain)
    v = v.reshape(bsz, seq, SWA_KV_HEADS, SWA_DH)

    def windows(t):
        tp = jnp.pad(t, ((0, 0), (WINDOW, WINDOW), (0, 0), (0, 0))).reshape(bsz, nb + 2 * nbr, Q_BLOCK, SWA_KV_HEADS, SWA_DH)
        return jnp.concatenate([tp[:, j:j + nb] for j in range(span)], axis=2)

    kw, vw = windows(k), windows(v)
    i = jnp.arange(Q_BLOCK)[:, None]
    j = jnp.arange(span * Q_BLOCK)[None, :]
    rel = j - WINDOW - i
    bias = rel_table[t5_bucket(rel)].transpose(2, 0, 1).reshape(SWA_KV_HEADS, grp, Q_BLOCK, span * Q_BLOCK).astype(F32)
    key_pos = jnp.arange(nb)[:, None, None] * Q_BLOCK + (j - WINDOW)[None]
    valid = (jnp.abs(rel) <= WINDOW)[None] & (key_pos >= 0) & (key_pos < seq)
    s = jnp.einsum('bnqhgd,bnkhd->bnhgqk', q, kw).astype(F32) * (SWA_DH ** -0.5) + bias[None, None]
    s = jnp.where(valid[None, :, None, None], s, -jnp.inf)
    sink_l = sink.astype(F32).reshape(SWA_KV_HEADS, grp)[None, None, :, :, None, None]
    mx = jnp.maximum(jnp.max(s, axis=-1, keepdims=True), sink_l)
    p = jnp.exp(s - mx)
    p = p / (jnp.sum(p, axis=-1, keepdims=True) + jnp.exp(sink_l - mx))
    o = jnp.einsum('bnhgqk,bnkhd->bnqhgd', p.astype(vw.dtype), vw).reshape(bsz, seq, SWA_HEADS, SWA_DH)
    o = o * jax.nn.silu(gate.reshape(bsz, seq, SWA_HEADS, SWA_DH))
    return o.reshape(bsz, seq, SWA_HEADS * SWA_DH)


def mla_attention(q, c_kv, k_rope, gate, positions, kv_gain, w_kv_up, q_gain, k_gain):
    bsz, seq, _ = q.shape
    nb = seq // Q_BLOCK
    dqk = MLA_NOPE + MLA_ROPE
    kv = (rms_norm(c_kv, kv_gain) @ w_kv_up).reshape(bsz, seq, MLA_HEADS, MLA_NOPE + MLA_DV)
    k_nope, v = kv[..., :MLA_NOPE], kv[..., MLA_NOPE:]
    k = jnp.concatenate([k_nope, jnp.broadcast_to(k_rope[:, :, None, :], (bsz, seq, MLA_HEADS, MLA_ROPE))], axis=-1)
    q = rms_norm(q.reshape(bsz, seq, MLA_HEADS, dqk), q_gain)
    k = rms_norm(k, k_gain)
    q = jnp.concatenate([q[..., :MLA_NOPE], rope(q[..., MLA_NOPE:], positions)], axis=-1)
    k = jnp.concatenate([k[..., :MLA_NOPE], rope(k[..., MLA_NOPE:], positions)], axis=-1)
    qb = q.reshape(bsz, nb, Q_BLOCK, MLA_HEADS, dqk).swapaxes(0, 1)

    def block(qblk):
        s = jnp.einsum('bqhd,bkhd->bhqk', qblk, k).astype(F32) * (dqk ** -0.5)
        p = jax.nn.softmax(s, axis=-1)
        return jnp.einsum('bhqk,bkhd->bqhd', p.astype(v.dtype), v)

    o = lax.map(block, qb).swapaxes(0, 1).reshape(bsz, seq, MLA_HEADS, MLA_DV)
    o = o * jax.nn.silu(gate.reshape(bsz, seq, MLA_HEADS, MLA_DV))
    return o.reshape(bsz, seq, MLA_HEADS * MLA_DV)


def memory_attention(q, gate, mem, mem_gain, w_kv, q_gain, k_gain):
    bsz, seq, _ = q.shape
    mlen = mem.shape[1]
    mk, mv = split_cols(rms_norm(mem, mem_gain) @ w_kv, [MEM_HEADS * MEM_DH, MEM_HEADS * MEM_DH])
    mk = rms_norm(mk.reshape(bsz, mlen, MEM_HEADS, MEM_DH), k_gain)
    mv = mv.reshape(bsz, mlen, MEM_HEADS, MEM_DH)
    q = rms_norm(q.reshape(bsz, seq, MEM_HEADS, MEM_DH), q_gain)
    s = jnp.einsum('bqhd,bmhd->bhqm', q, mk).astype(F32) * (MEM_DH ** -0.5)
    p = jax.nn.softmax(s, axis=-1)
    o = jnp.einsum('bhqm,bmhd->bqhd', p.astype(mv.dtype), mv)
    o = o * jax.nn.silu(gate.reshape(bsz, seq, MEM_HEADS, MEM_DH))
    return o.reshape(bsz, seq, MEM_HEADS * MEM_DH)


def even_layer(x, mem, rel_table, norm_g, w_in, conv_w, a_log, dt_bias, gdn_gain, dq_gain, dk_gain, lam, subln,
               mem_norm, mem_w_kv, mem_qn, mem_kn, w_out, lambda_init):
    h = rms_norm(x, norm_g) @ w_in
    gq, gk, gv, gb, ga, gg, dq, dk, dv, dg, mq, mg = split_cols(h, EV_SPLITS)
    y = jnp.concatenate([
        gdn_branch(gq, gk, gv, gb, ga, gg, conv_w, a_log, dt_bias, gdn_gain),
        diff_attention(dq, dk, dv, dg, rel_table, dq_gain, dk_gain, lam, subln, lambda_init),
        memory_attention(mq, mg, mem, mem_norm, mem_w_kv, mem_qn, mem_kn),
    ], axis=-1)
    return x + y @ w_out


def odd_layer(x, mem, positions, rel_table, norm_g, w_in, swa_qn, swa_kn, sink, kv_norm, w_kv_up, mla_qn, mla_kn,
              mem_norm, mem_w_kv, mem_qn, mem_kn, w_out):
    h = rms_norm(x, norm_g) @ w_in
    sq, sk, sv, sg, mlq, ckv, kr, mlg, mq, mg = split_cols(h, OD_SPLITS)
    y = jnp.concatenate([
        window_attention(sq, sk, sv, sg, rel_table, swa_qn, swa_kn, sink),
        mla_attention(mlq, ckv, kr, mlg, positions, kv_norm, w_kv_up, mla_qn, mla_kn),
        memory_attention(mq, mg, mem, mem_norm, mem_w_kv, mem_qn, mem_kn),
    ], axis=-1)
    return x + y @ w_out


def setup_inputs(seed: int = 0) -> dict:
    key = jax.random.key(seed)
    keys = jax.random.split(key, 33)

    def nrm(i, shape, scale):
        return scale * jax.random.normal(keys[i], shape, F32)

    def gain(i, shape):
        return 1.0 + 0.02 * jax.random.normal(keys[i], shape, F32)

    res = (2.0 * DEPTH) ** -0.5
    offsets = jax.random.randint(keys[2], (BATCH, 1), 0, 4096, dtype=jnp.int32)
    positions = offsets + jnp.arange(SEQ, dtype=jnp.int32)[None, :]
    a_log = jnp.log(jax.random.uniform(keys[7], (N_EVEN, 2, GDN_HEADS), F32, 1.0, 16.0))
    dt = jnp.exp(jax.random.uniform(keys[8], (N_EVEN, 2, GDN_HEADS), F32, math.log(1e-3), math.log(1e-1)))
    dt_bias = dt + jnp.log(-jnp.expm1(-dt))
    return {
        'x': nrm(0, (BATCH, SEQ, D_MODEL), 1.0),
        'mem': nrm(1, (BATCH, MEM_LEN, D_MODEL), 1.0),
        'positions': positions,
        'rel_bias': nrm(3, (REL_BUCKETS, REL_HEADS), 0.5),
        'ev_norm': gain(4, (N_EVEN, D_MODEL)),
        'ev_w_in': nrm(5, (N_EVEN, D_MODEL, EV_IN), D_MODEL ** -0.5),
        'ev_conv': nrm(6, (N_EVEN, GDN_CONV, GDN_CONV_CH), GDN_CONV ** -0.5),
        'ev_a_log': a_log,
        'ev_dt_bias': dt_bias,
        'ev_gdn_norm': gain(9, (N_EVEN, GDN_DV)),
        'ev_diff_qnorm': gain(10, (N_EVEN, DIFF_DQK)),
        'ev_diff_knorm': gain(11, (N_EVEN, DIFF_DQK)),
        'ev_diff_lambda': nrm(12, (N_EVEN, 4, DIFF_DQK), 0.1),
        'ev_diff_subln': gain(13, (N_EVEN, DIFF_DV)),
        'ev_mem_norm': gain(14, (N_EVEN, D_MODEL)),
        'ev_mem_w_kv': nrm(15, (N_EVEN, D_MODEL, 2 * MEM_HEADS * MEM_DH), D_MODEL ** -0.5),
        'ev_mem_qnorm': gain(16, (N_EVEN, MEM_DH)),
        'ev_mem_knorm': gain(17, (N_EVEN, MEM_DH)),
        'ev_w_out': nrm(18, (N_EVEN, EV_OUT, D_MODEL), res * EV_OUT ** -0.5),
        'od_norm': gain(19, (N_ODD, D_MODEL)),
        'od_w_in': nrm(20, (N_ODD, D_MODEL, OD_IN), D_MODEL ** -0.5),
        'od_swa_qnorm': gain(21, (N_ODD, SWA_DH)),
        'od_swa_knorm': gain(22, (N_ODD, SWA_DH)),
        'od_swa_sink': nrm(23, (N_ODD, SWA_HEADS), 0.5),
        'od_mla_kv_norm': gain(24, (N_ODD, MLA_KV_RANK)),
        'od_mla_w_kv_up': nrm(25, (N_ODD, MLA_KV_RANK, MLA_HEADS * (MLA_NOPE + MLA_DV)), MLA_KV_RANK ** -0.5),
        'od_mla_qnorm': gain(26, (N_ODD, MLA_NOPE + MLA_ROPE)),
        'od_mla_knorm': gain(27, (N_ODD, MLA_NOPE + MLA_ROPE)),
        'od_mem_norm': gain(28, (N_ODD, D_MODEL)),
        'od_mem_w_kv': nrm(29, (N_ODD, D_MODEL, 2 * MEM_HEADS * MEM_DH), D_MODEL ** -0.5),
        'od_mem_qnorm': gain(30, (N_ODD, MEM_DH)),
        'od_mem_knorm': gain(31, (N_ODD, MEM_DH)),
        'od_w_out': nrm(32, (N_ODD, OD_OUT, D_MODEL), res * OD_OUT ** -0.5),
    }


def reference(x, mem, positions, rel_bias,
              ev_norm, ev_w_in, ev_conv, ev_a_log, ev_dt_bias, ev_gdn_norm, ev_diff_qnorm, ev_diff_knorm,
              ev_diff_lambda, ev_diff_subln, ev_mem_norm, ev_mem_w_kv, ev_mem_qnorm, ev_mem_knorm, ev_w_out,
              od_norm, od_w_in, od_swa_qnorm, od_swa_knorm, od_swa_sink, od_mla_kv_norm, od_mla_w_kv_up,
              od_mla_qnorm, od_mla_knorm, od_mem_norm, od_mem_w_kv, od_mem_qnorm, od_mem_knorm, od_w_out):
    for layer in range(DEPTH):
        i = layer // 2
        if layer % 2 == 0:
            lambda_init = 0.8 - 0.6 * math.exp(-0.3 * layer)
            x = even_layer(x, mem, rel_bias, ev_norm[i], ev_w_in[i], ev_conv[i], ev_a_log[i], ev_dt_bias[i],
                           ev_gdn_norm[i], ev_diff_qnorm[i], ev_diff_knorm[i], ev_diff_lambda[i], ev_diff_subln[i],
                           ev_mem_norm[i], ev_mem_w_kv[i], ev_mem_qnorm[i], ev_mem_knorm[i], ev_w_out[i], lambda_init)
        else:
            x = odd_layer(x, mem, positions, rel_bias, od_norm[i], od_w_in[i], od_swa_qnorm[i], od_swa_knorm[i],
                          od_swa_sink[i], od_mla_kv_norm[i], od_mla_w_kv_up[i], od_mla_qnorm[i], od_mla_knorm[i],
                          od_mem_norm[i], od_mem_w_kv[i], od_mem_qnorm[i], od_mem_knorm[i], od_w_out[i])
    return x
```

```python
from contextlib import ExitStack, contextmanager
import numpy as np
import concourse.bass as bass
import concourse.mybir as mybir
from concourse.bass_utils import run_bass_kernel_spmd

F32 = mybir.dt.float32
BF16 = mybir.dt.bfloat16
I32 = mybir.dt.int32
AF = mybir.ActivationFunctionType
ALU = mybir.AluOpType
AX = mybir.AxisListType

N_DMA_SEMS = {'sp': 12, 'act': 6, 'pool': 6}


class Prog:
    COMPUTE = ['pe', 'act', 'dve', 'pool']

    def __init__(self, nc):
        self.nc = nc
        self.root = ExitStack()
        self.handles = {'pe': nc.tensor, 'act': nc.scalar, 'dve': nc.vector, 'pool': nc.gpsimd, 'sp': nc.sync}
        self.sem = {e: self.root.enter_context(nc.semaphore('s_' + e)) for e in self.COMPUTE}
        self.cnt = {e: 0 for e in self.COMPUTE}
        self.dsem = {q: [self.root.enter_context(nc.semaphore('d_%s%d' % (q, i))) for i in range(n)]
                     for q, n in N_DMA_SEMS.items()}
        self.dcnt = {q: 0 for q in N_DMA_SEMS}
        self.waited = {e: {} for e in self.handles}
        self.ops = []
        self.phase_stack = None
        self.barrier = {}
        self.final_waits = []
        self.nops = 0

    def sbuf(self, name, shape, dtype, persistent=False):
        st = self.root if (persistent or self.phase_stack is None) else self.phase_stack
        self.uid = getattr(self, 'uid', 0) + 1
        return st.enter_context(self.nc.sbuf_tensor('%s_%d' % (name, self.uid), shape, dtype))

    def psum(self, name, shape, dtype, persistent=False):
        st = self.root if (persistent or self.phase_stack is None) else self.phase_stack
        self.uid = getattr(self, 'uid', 0) + 1
        return st.enter_context(self.nc.psum_tensor('%s_%d' % (name, self.uid), shape, dtype))

    @contextmanager
    def phase(self, name=''):
        assert self.phase_stack is None
        with ExitStack() as st:
            self.phase_stack = st
            yield
            self.emit()
            self.phase_stack = None

    def op(self, eng, fn, reads=(), writes=()):
        self.ops.append(dict(eng=eng, fn=fn, reads=list(reads), writes=list(writes), dma=False, final=False))

    def dma(self, q, fn, reads=(), writes=(), final=False):
        self.ops.append(dict(eng=q, fn=fn, reads=list(reads), writes=list(writes), dma=True, final=final))

    def _semobj(self, key):
        return self.sem[key] if isinstance(key, str) else self.dsem[key[0]][key[1]]

    def emit(self):
        ops = self.ops
        self.ops = []
        n = len(ops)
        if n == 0:
            return
        self.nops += n
        last_write, readers = {}, {}
        deps = [None] * n
        needs_sig = [False] * n
        for i, o in enumerate(ops):
            d = set()
            for k in o['reads']:
                if k in last_write:
                    d.add(last_write[k])
                    lw = ops[last_write[k]]
                    if lw['eng'] == 'pe' and not lw['dma']:
                        d.update(readers.get(k, ()))
            for k in o['writes']:
                if k in last_write:
                    d.add(last_write[k])
                d.update(readers.get(k, ()))
            d.discard(i)
            d = {j for j in d if not (ops[j]['eng'] == 'pe' and o['eng'] == 'pe' and not ops[j]['dma'] and not o['dma'])}
            deps[i] = sorted(d)
            for j in d:
                needs_sig[j] = True
            for k in o['reads']:
                readers.setdefault(k, []).append(i)
            for k in o['writes']:
                last_write[k] = i
                readers[k] = []
        last_on = {}
        for i, o in enumerate(ops):
            if not o['dma']:
                last_on[o['eng']] = i
        for e, i in last_on.items():
            needs_sig[i] = True
        sig = [None] * n
        pre_wait = [None] * n
        for i, o in enumerate(ops):
            if o['dma']:
                q = o['eng']
                j = self.dcnt[q]
                self.dcnt[q] += 1
                ns = len(self.dsem[q])
                key = (q, j % ns)
                val = 16 * (j // ns + 1)
                sig[i] = (key, val)
                if val > 16:
                    pre_wait[i] = (key, val - 16)
            elif needs_sig[i]:
                self.cnt[o['eng']] += 1
                sig[i] = (o['eng'], self.cnt[o['eng']])
        streams = {e: [] for e in self.handles}
        for i, o in enumerate(ops):
            streams[o['eng']].append(i)
        barrier = dict(self.barrier)
        waited = self.waited

        def need_wait(e, key, val):
            if waited[e].get(key, 0) >= val:
                return False
            waited[e][key] = val
            return True

        def run_stream(e, h):
            first = True
            for i in streams[e]:
                o = ops[i]
                waits = []
                if first:
                    first = False
                    for key, val in barrier.items():
                        if need_wait(e, key, val):
                            waits.append((key, val))
                if pre_wait[i] is not None and need_wait(e, *pre_wait[i]):
                    waits.append(pre_wait[i])
                for j in deps[i]:
                    key, val = sig[j]
                    if need_wait(e, key, val):
                        waits.append((key, val))
                for key, val in waits:
                    h.wait_ge(self._semobj(key), val)
                ins = o['fn'](h)
                if sig[i] is not None:
                    key, val = sig[i]
                    ins.then_inc(self._semobj(key), 16 if o['dma'] else 1)
                    if o['final']:
                        self.final_waits.append((key, val))

        with self.nc.Block() as block:
            for e, regf in (('sp', block.sync), ('act', block.scalar), ('dve', block.vector),
                            ('pool', block.gpsimd), ('pe', block.tensor)):
                if streams[e]:
                    regf(lambda h, e=e: run_stream(e, h))
        nb = {}
        for e in self.COMPUTE:
            if self.cnt[e]:
                nb[e] = self.cnt[e]
        for q, sems in self.dsem.items():
            ns = len(sems)
            for s in range(ns):
                issued = (self.dcnt[q] - s + ns - 1) // ns if self.dcnt[q] > s else 0
                if issued:
                    nb[(q, s)] = 16 * issued
        self.barrier = nb

    def finish(self):
        assert not self.ops
        barrier = dict(self.barrier)
        with self.nc.Block() as block:
            def fin(h):
                for key, val in barrier.items():
                    h.wait_ge(self._semobj(key), val)
            block.sync(fin)
        self.root.close()

D = 2048
NCH = D // 128
EPS = 1e-6
NEG = -1.0e5
EV_OFF = dict(gq=0, gk=1024, gv=2048, gb=3072, ga=3088, gg=3104, dq=4128, dk=5152, dv=6176, dg=7200,
              mq=8224, mg=8736)
OD_OFF = dict(sq=0, sk=1024, sv=1280, sg=1536, mlq=2560, ckv=4096, kr=4608, mlg=4672, mq=5696, mg=6208)
TOEP_W = 1152


def _t5_bucket_np(rel):
    import math
    half, me = 16, 8
    ret = (rel > 0).astype(np.int32) * half
    n = np.abs(rel)
    nf = np.maximum(n, 1).astype(np.float32)
    large = me + (np.log(nf / np.float32(me)) / np.float32(math.log(128 / me)) * np.float32(half - me)).astype(np.int32)
    large = np.minimum(large, half - 1)
    return ret + np.where(n < me, n, large)


def const_mats():
    r = np.arange(128)[:, None]
    c = np.arange(128)[None, :]
    mats = dict(
        ident=(r == c), ones=np.ones((128, 128)),
        tri_f=(r <= c), tri_b=(r >= c), tric_f=(r > c), tric_b=(r < c),
        negs_f=np.where(r > c, 0.0, NEG), negs_b=np.where(r < c, 0.0, NEG),
        negi_f=np.where(r <= c, 0.0, NEG), negi_b=np.where(r >= c, 0.0, NEG),
        blk64=((r // 64) == (c // 64)), negones=-np.ones((128, 128)),
        selA=(r < 64) + 0 * c, selB=(r >= 64) + 0 * c,
        rot=np.where((r // 64 == c // 64) & (r % 64 == c % 64 + 32), -1.0, 0.0) + np.where((r // 64 == c // 64) & (r % 64 + 32 == c % 64), 1.0, 0.0),
    )
    names = list(mats)
    arr = np.stack([np.asarray(mats[k], np.float32) for k in names], axis=1)
    return names, np.ascontiguousarray(arr)


CM_NAMES, CM_ARR = const_mats()


def fm_weight(w, cols):
    cols = np.asarray(cols)
    nft = len(cols) // 128
    k = w.shape[0] // 128
    ws = w[:, cols]
    ws = ws.reshape(k, 128, nft, 128).transpose(2, 1, 0, 3)
    return np.ascontiguousarray(ws)


def colvec(v):
    v = np.asarray(v, np.float32)
    return np.ascontiguousarray(v.reshape(-1, 128).T)


def bcast(v):
    v = np.asarray(v, np.float32).reshape(1, -1)
    return np.ascontiguousarray(np.broadcast_to(v, (128, v.shape[1])))


class CV:
    def __init__(self):
        self.parts, self.off, self.n = [], {}, 0

    def add(self, name, arr):
        arr = np.asarray(arr, np.float32)
        assert arr.shape[0] == 128
        self.off[name] = (self.n, arr.shape[1])
        self.parts.append(arr)
        self.n += arr.shape[1]

    def arr(self):
        return np.ascontiguousarray(np.concatenate(self.parts, axis=1))


class Ring:
    def __init__(self, items):
        self.items, self.i = items, 0

    def next(self):
        it = self.items[self.i % len(self.items)]
        self.i += 1
        return it


class B:
    def __init__(self, P):
        self.P = P

    def mm(self, out, lhsT, rhs, r, w, start=True, stop=True):
        self.P.op('pe', lambda e: e.matmul(out, lhsT=lhsT, rhs=rhs, start=start, stop=stop), r, w)

    def tr(self, out, in_, ident, r, w):
        self.P.op('pe', lambda e: e.transpose(out, in_, ident), r, w)

    def act(self, out, in_, func, r, w, bias=None, scale=None, accum=None):
        kw = {}
        if bias is not None:
            kw['bias'] = bias
        if scale is not None:
            kw['scale'] = scale
        if accum is not None:
            kw['accum_out'] = accum
        self.P.op('act', lambda e: e.activation(out=out, in_=in_, func=func, **kw), r, w)

    def tt(self, eng, out, in0, in1, op, r, w):
        self.P.op(eng, lambda e: e.tensor_tensor(out=out, in0=in0, in1=in1, op=op), r, w)

    def ts(self, eng, out, in0, s1, op0, r, w, s2=None, op1=None):
        if op1 is None:
            self.P.op(eng, lambda e: e.tensor_scalar(out=out, in0=in0, scalar1=s1, scalar2=None, op0=op0), r, w)
        else:
            self.P.op(eng, lambda e: e.tensor_scalar(out=out, in0=in0, scalar1=s1, scalar2=s2, op0=op0, op1=op1), r, w)

    def stt(self, eng, out, in0, scalar, in1, op0, op1, r, w):
        eng = 'dve'
        self.P.op(eng, lambda e: e.scalar_tensor_tensor(out=out, in0=in0, scalar=scalar, in1=in1, op0=op0, op1=op1), r, w)

    def copy(self, eng, out, in_, r, w):
        if eng == 'act':
            self.P.op('act', lambda e: e.copy(out=out, in_=in_), r, w)
        else:
            self.P.op(eng, lambda e: e.tensor_copy(out=out, in_=in_), r, w)

    def recip(self, out, in_, r, w, eng='dve'):
        self.P.op(eng, lambda e: e.reciprocal(out=out, in_=in_), r, w)

    def memset(self, eng, ap, val, w):
        self.P.op(eng, lambda e: e.memset(ap, val), [], w)

    def dma(self, q, out, in_, r, w, final=False):
        self.P.dma(q, lambda e: e.dma_start(out=out, in_=in_), r, w, final=final)


def load_consts(P, bb, cmat_d, cvec_d, ncv):
    K = len(CM_NAMES)
    cm = P.sbuf('cm', [128, K, 128], F32, persistent=True)
    cmb = P.sbuf('cmb', [128, K, 128], BF16, persistent=True)
    cv = P.sbuf('cv', [128, ncv], F32, persistent=True)
    bb.dma('sp', cm[:], cmat_d, [], ['cm'])
    bb.dma('sp', cv[:], cvec_d, [], ['cv'])
    bb.copy('dve', cmb[:], cm[:], ['cm'], ['cmb'])
    idx = {n: i for i, n in enumerate(CM_NAMES)}
    return (lambda n: cm[:, idx[n], :]), (lambda n: cmb[:, idx[n], :]), cv


def phase_xnT(P, bb, T, x_rows, xnT, ident_bf):
    with P.phase():
        xs = Ring([(P.sbuf('xs%d' % i, [128, D], F32), 'xs%d' % i) for i in range(2)])
        xb = Ring([(P.sbuf('xb%d' % i, [128, D], BF16), 'xb%d' % i) for i in range(2)])
        junk = P.sbuf('junk', [128, D], BF16)
        st = Ring([(P.sbuf('st%d' % i, [128, 4], F32), 'st%d' % i) for i in range(3)])
        pst = Ring([(P.psum('pst%d' % i, [128, 1024], BF16), 'pst%d' % i) for i in range(4)])
        for tt in range(T // 128):
            xt, kx = xs.next()
            xbt, kb = xb.next()
            s, ks = st.next()
            bb.dma('sp', xt[:], x_rows(tt), [], [kx])
            bb.act(junk[:], xt[:], AF.Square, [kx], [ks + 'a'], accum=s[:, 0:1])
            bb.ts('dve', s[:, 1:2], s[:, 0:1], 1.0 / D, ALU.mult, [ks + 'a'], [ks + 'b'], s2=EPS, op1=ALU.add)
            bb.act(s[:, 2:3], s[:, 1:2], AF.Sqrt, [ks + 'b'], [ks + 'c'])
            bb.recip(s[:, 3:4], s[:, 2:3], [ks + 'c'], [ks + 'd'])
            bb.ts('dve', xbt[:], xt[:], s[:, 3:4], ALU.mult, [kx, ks + 'd'], [kb])
            for half in range(2):
                pt, kp = pst.next()
                for j in range(8):
                    c = half * 8 + j
                    bb.tr(pt[:, j * 128:(j + 1) * 128], xbt[:, c * 128:(c + 1) * 128], ident_bf, [kb], [kp])
                bb.copy('act' if half == 0 else 'dve', xnT[:, half * 8:(half + 1) * 8, tt * 128:(tt + 1) * 128],
                        pt[:].rearrange('p (c t) -> p c t', c=8), [kp], [])


def phase_proj(P, bb, T, xnT, gcol, w_d, specs, cmb):
    NQ = T // 512
    with P.phase():
        wst = Ring([(P.sbuf('wst%d' % i, [128, NCH, 128], F32), 'wst%d' % i) for i in range(2)])
        wbr = Ring([(P.sbuf('wbr%d' % i, [128, NCH, 128], BF16), 'wbr%d' % i) for i in range(5)])
        psA = Ring([(P.psum('psA%d' % i, [128, 512], F32), 'psA%d' % i) for i in range(5)])
        psB = Ring([(P.psum('psB%d' % i, [128, 512], F32), 'psB%d' % i) for i in range(3)])
        rf = Ring([(P.sbuf('rf%d' % i, [128, 512], F32), 'rf%d' % i) for i in range(4)])
        rb = Ring([(P.sbuf('rb%d' % i, [128, 512], BF16), 'rb%d' % i) for i in range(6)])
        i = -1
        pend = []
        for sp in specs:
            i += 1
            wt, kw = wst.next()
            wbt, kwb = wbr.next()
            bb.dma('sp', wt[:], w_d[i].rearrange('p (c n) -> p c n', c=NCH), [], [kw])
            g = sp.get('gcol', gcol)
            bb.tt('dve' if i % 2 == 0 else 'pool', wbt[:], wt[:], g.unsqueeze(2).to_broadcast([128, NCH, 128]), ALU.mult,
                  [kw], [kwb])
            kind = sp['kind']
            if kind == 'hold':
                pend.append((wbt, kwb))
                continue
            if kind in ('mlaq', 'ckv'):
                pend.append((wbt, kwb))
                for qb in range(NQ):
                    cs = slice(qb * 512, (qb + 1) * 512)
                    pss = []
                    for (wb_, kwb_) in pend:
                        ps, kp = psA.next()
                        for c in range(NCH):
                            bb.mm(ps[:], wb_[:, c, :], xnT[:, c, cs], [kwb_], [kp], start=(c == 0), stop=(c == NCH - 1))
                        pss.append((ps, kp))
                    sqs = []
                    for (ps, kp) in pss:
                        sq, ksq = rb.next()
                        bb.act(sq[:], ps[:], AF.Square, [kp], [ksq])
                        sqs.append((sq, ksq))
                    if kind == 'ckv':
                        p2, kp2 = psB.next()
                        for j, (sq, ksq) in enumerate(sqs):
                            bb.mm(p2[:], cmb('ones'), sq[:], [ksq], [kp2], start=(j == 0), stop=(j == 3))
                        rt, krt = rf.next()
                        bb.ts('dve', rt[:], p2[:], 1.0 / 512, ALU.mult, [kp2], [krt], s2=EPS, op1=ALU.add)
                        bb.act(rt[:], rt[:], AF.Sqrt, [krt], [krt])
                        bb.recip(rt[:], rt[:], [krt], [krt])
                        for j, (ps, kp) in enumerate(pss):
                            o, ko = rb.next()
                            bb.stt('dve', o[:], ps[:], sp['gains'][:, j:j + 1], rt[:], ALU.mult, ALU.mult, [kp, krt], [ko])
                            bb.dma('pool', sp['dst'](j, qb), o[:], [ko], [])
                    else:
                        rts = []
                        for j in range(2):
                            p2, kp2 = psB.next()
                            bb.mm(p2[:], cmb('ones'), sqs[j][0][:], [sqs[j][1]], [kp2], start=True, stop=False)
                            bb.mm(p2[:], cmb('selA' if j == 0 else 'selB'), sqs[2][0][:], [sqs[2][1]], [kp2], start=False, stop=True)
                            rt, krt = rf.next()
                            bb.ts('dve', rt[:], p2[:], 1.0 / 192, ALU.mult, [kp2], [krt], s2=EPS, op1=ALU.add)
                            bb.act(rt[:], rt[:], AF.Sqrt, [krt], [krt])
                            bb.recip(rt[:], rt[:], [krt], [krt])
                            rts.append((rt, krt))
                            o, ko = rb.next()
                            bb.stt('dve', o[:], pss[j][0][:], sp['combn'], rt[:], ALU.mult, ALU.mult, [pss[j][1], krt], [ko])
                            bb.dma('pool', sp['dstn'](j, qb), o[:], [ko], [])
                        o, ko = rb.next()
                        for j in range(2):
                            rws = slice(64 * j, 64 * j + 64)
                            bb.stt('dve', o[rws, :], pss[2][0][rws, :], sp['gqr'][rws, :], rts[j][0][rws, :], ALU.mult, ALU.mult,
                                   [pss[2][1], rts[j][1]], [ko])
                        bb.dma('pool', sp['dstr'](qb), o[:], [ko], [])
                pend = []
                continue
            for qb in range(NQ):
                ps, kp = psA.next()
                cs = slice(qb * 512, (qb + 1) * 512)
                if kind.startswith('tm'):
                    for s in range(4):
                        for c in range(NCH):
                            bb.mm(ps[:, s * 128:(s + 1) * 128], xnT[:, c, qb * 512 + s * 128:qb * 512 + (s + 1) * 128],
                                  wbt[:, c, :], [kwb], [kp], start=(c == 0), stop=(c == NCH - 1))
                    nc_ = sp['ncols']
                    o, ko = (rb if sp['dt'] == 'bf16' else rf).next()
                    ov = o[:].rearrange('p (s n) -> p s n', s=4)[:, :, 0:nc_]
                    bb.copy('act', ov, ps[:].rearrange('p (s n) -> p s n', s=4)[:, :, 0:nc_], [kp], [ko])
                    bb.dma('pool', sp['dst'](qb), ov, [ko], [])
                    continue
                for c in range(NCH):
                    bb.mm(ps[:], wbt[:, c, :], xnT[:, c, cs], [kwb], [kp], start=(c == 0), stop=(c == NCH - 1))
                if kind == 'raw32':
                    o, ko = rf.next()
                    bb.copy('act', o[:], ps[:], [kp], [ko])
                    bb.dma('pool', sp['dst'](qb), o[:], [ko], [])
                elif kind == 'silu':
                    o, ko = rb.next()
                    bb.act(o[:], ps[:], AF.Silu, [kp], [ko])
                    bb.dma('pool', sp['dst'](qb), o[:], [ko], [])
                elif kind == 'rawbf':
                    o, ko = rb.next()
                    bb.copy('act', o[:], ps[:], [kp], [ko])
                    bb.dma('pool', sp['dst'](qb), o[:], [ko], [])
                elif kind == 'qnorm':
                    gs = sp['gs']
                    sq, ksq = rb.next()
                    bb.act(sq[:], ps[:], AF.Square, [kp], [ksq])
                    p2, kp2 = psB.next()
                    bb.mm(p2[:], cmb('blk64' if gs == 64 else 'ones'), sq[:], [ksq], [kp2])
                    rt, krt = rf.next()
                    bb.ts('dve', rt[:], p2[:], 1.0 / gs, ALU.mult, [kp2], [krt], s2=EPS, op1=ALU.add)
                    bb.act(rt[:], rt[:], AF.Sqrt, [krt], [krt])
                    bb.recip(rt[:], rt[:], [krt], [krt])
                    o, ko = rb.next()
                    bb.stt('dve', o[:], ps[:], sp['comb'], rt[:], ALU.mult, ALU.mult, [kp, krt], [ko])
                    bb.dma('pool', sp['dst'](qb), o[:], [ko], [])
                elif kind == 'kraw':
                    gs = sp['gs']
                    ng = 128 // gs
                    o, ko = rb.next()
                    bb.copy('act', o[:], ps[:], [kp], [ko])
                    bb.dma('pool', sp['dst'](qb), o[:], [ko], [])
                    sq, ksq = rb.next()
                    bb.act(sq[:], ps[:], AF.Square, [kp], [ksq])
                    p2, kp2 = psB.next()
                    sel = cmb('blk64')[:, 0:128:64] if gs == 64 else cmb('ones')[:, 0:1]
                    for s in range(4):
                        bb.mm(p2[:, s * ng:(s + 1) * ng], sq[:, s * 128:(s + 1) * 128], sel, [ksq], [kp2])
                    rt, krt = rf.next()
                    bb.ts('dve', rt[:, 0:4 * ng], p2[:, 0:4 * ng], 1.0 / gs, ALU.mult, [kp2], [krt], s2=EPS, op1=ALU.add)
                    bb.act(rt[:, 0:4 * ng], rt[:, 0:4 * ng], AF.Sqrt, [krt], [krt])
                    rk = sp['rk'](qb)
                    bb.recip(rk, rt[:, 0:4 * ng].rearrange('p (s g) -> p s g', s=4), [krt], [])
                else:
                    raise ValueError(kind)


def prep_A(inp, core):
    b, hh = core // 2, core % 2
    w = np.asarray(inp['ev_w_in'][0], np.float32)
    cols = []
    for nm in ('gq', 'gk', 'gv', 'gg'):
        cols += list(range(EV_OFF[nm] + 4 * hh * 128, EV_OFF[nm] + (4 * hh + 4) * 128))
    for nm in ('dq', 'dk', 'dg'):
        cols += list(range(EV_OFF[nm] + 4 * hh * 128, EV_OFF[nm] + (4 * hh + 4) * 128))
    for nm in ('mq', 'mg'):
        cols += list(range(EV_OFF[nm] + 2 * hh * 128, EV_OFF[nm] + (2 * hh + 2) * 128))
    cols += list(range(EV_OFF['dv'] + 4 * hh * 128, EV_OFF['dv'] + (4 * hh + 4) * 128))
    ba = []
    for nm in ('gb', 'ga'):
        for d in range(2):
            ba += [EV_OFF[nm] + d * 8 + 4 * hh + h for h in range(4)]
    cols += ba + [ba[0]] * (128 - 16)
    wfm = fm_weight(w, cols).reshape(-1, 128, NCH * 128)
    cv = CV()
    cv.add('gnorm', colvec(inp['ev_norm'][0]))
    cv.add('gmem', colvec(inp['ev_mem_norm'][0]))
    conv = np.asarray(inp['ev_conv'][0], np.float32)
    for part in range(3):
        for h in range(4):
            ch = part * 1024 + (4 * hh + h) * 128 + np.arange(128)
            cv.add('conv%d_%d' % (part, h), conv[:, ch].T)
    cv.add('dqg', np.tile(np.asarray(inp['ev_diff_qnorm'][0]), 2)[:, None])
    cv.add('dkg', np.tile(np.asarray(inp['ev_diff_knorm'][0]), 2)[:, None])
    cv.add('subln', np.asarray(inp['ev_diff_subln'][0])[:, None])
    cv.add('gdng', bcast(inp['ev_gdn_norm'][0]))
    cv.add('mqg', np.asarray(inp['ev_mem_qnorm'][0])[:, None])
    cv.add('mkg', np.asarray(inp['ev_mem_knorm'][0])[:, None])
    al = np.asarray(inp['ev_a_log'][0])[:, 4 * hh:4 * hh + 4].reshape(-1)
    dtb = np.asarray(inp['ev_dt_bias'][0])[:, 4 * hh:4 * hh + 4].reshape(-1)
    cv.add('alog', bcast(al))
    cv.add('dtb', bcast(dtb))
    cv.add('lam', bcast(np.asarray(inp['ev_diff_lambda'][0]).reshape(-1)))
    rb_ = np.asarray(inp['rel_bias'], np.float32)
    heads = [4 * hh + h for h in range(4)]
    cv.add('bfar', bcast(np.stack([rb_[15, heads], rb_[31, heads]], axis=1).reshape(-1)))
    p = np.arange(128)[:, None]
    c = np.arange(TOEP_W)[None, :]
    bidx = _t5_bucket_np(p - c + 512)
    toep = np.stack([rb_[bidx, H] for H in heads], axis=1)
    mem = np.asarray(inp['mem'][b], np.float32)
    wkv = np.asarray(inp['ev_mem_w_kv'][0], np.float32)
    mcols = list(range(2 * hh * 128, (2 * hh + 2) * 128)) + list(range(512 + 2 * hh * 128, 512 + (2 * hh + 2) * 128))
    wmem = fm_weight(wkv, mcols).reshape(-1, 128, NCH * 128)
    return dict(x=np.ascontiguousarray(inp['x'][b]), wfm=wfm, cvec=cv.arr(), cmat=CM_ARR,
                toep=np.ascontiguousarray(toep.astype(np.float32)), mem=mem, wmem=wmem), cv.off


def build_A(T, cvoff, ncv, debug=False, phases=('proj', 'gdn', 'diff', 'mem')):
    nc = bass.Bass("TRN2", target_bir_lowering=False)
    NT = T // 128
    kind_s = "ExternalOutput" if debug else "Internal"
    x_d = nc.dram_tensor("x", [T, D], F32, kind="ExternalInput").ap()
    wfm_d = nc.dram_tensor("wfm", [37, 128, D], F32, kind="ExternalInput").ap()
    cvec_d = nc.dram_tensor("cvec", [128, ncv], F32, kind="ExternalInput").ap()
    cmat_d = nc.dram_tensor("cmat", [128, len(CM_NAMES), 128], F32, kind="ExternalInput").ap()
    toep_d = nc.dram_tensor("toep", [128, 4, TOEP_W], F32, kind="ExternalInput").ap()
    mem_d = nc.dram_tensor("mem", [256, D], F32, kind="ExternalInput").ap()
    wmem_d = nc.dram_tensor("wmem", [4, 128, D], F32, kind="ExternalInput").ap()
    y_d = nc.dram_tensor("yT", [1280, T], BF16, kind="ExternalOutput").ap()
    g_raw = nc.dram_tensor("g_raw", [12, 128, T + 4], F32, kind=kind_s).ap()
    gba_d = nc.dram_tensor("gba", [T, 16], F32, kind=kind_s).ap()
    gate_d = nc.dram_tensor("gateT", [10, 128, T], BF16, kind=kind_s).ap()
    dq_d = nc.dram_tensor("dqT", [4, 128, T], BF16, kind=kind_s).ap()
    dk_d = nc.dram_tensor("dkT", [4, 128, T], BF16, kind=kind_s).ap()
    dv_d = nc.dram_tensor("dv_tok", [T, 512], BF16, kind=kind_s).ap()
    mq_d = nc.dram_tensor("mqT", [2, 128, T], BF16, kind=kind_s).ap()
    rkdbg_d = nc.dram_tensor("rkdbg", [128, NT * 8], F32, kind=kind_s).ap()
    P = Prog(nc)
    bb = B(P)
    cm, cmb, cv = load_consts(P, bb, cmat_d, cvec_d, ncv)

    def cvc(name, j=0, n=1):
        o, w = cvoff[name]
        return cv[:, o + j:o + j + n]
    rk_diff = P.sbuf('rk_diff', [128, NT, 8], F32, persistent=True)
    comb = P.sbuf('comb', [128, 4], F32, persistent=True)
    bb.stt('dve', comb[:, 0:1], cvc('dqg'), 64 ** -0.5, cvc('dkg'), ALU.mult, ALU.mult, ['cv'], ['comb'])
    bb.stt('dve', comb[:, 1:2], cvc('mqg'), 128 ** -0.5, cvc('mkg'), ALU.mult, ALU.mult, ['cv'], ['comb'])
    P.emit()
    if 'proj' in phases:
        with ExitStack() as st:
            xnT = st.enter_context(nc.sbuf_tensor('xnT', [128, NCH, T], BF16))
            phase_xnT(P, bb, T, lambda tt: x_d[tt * 128:(tt + 1) * 128, :], xnT, cmb('ident'))
            gcol = cv[:, cvoff['gnorm'][0]:cvoff['gnorm'][0] + NCH]
            specs = []
            for i in range(12):
                specs.append(dict(kind='raw32', dst=lambda qb, i=i: g_raw[i, :, 2 + qb * 512:2 + (qb + 1) * 512]))
            for h in range(4):
                specs.append(dict(kind='silu', dst=lambda qb, h=h: gate_d[h, :, qb * 512:(qb + 1) * 512]))
            for h in range(4):
                specs.append(dict(kind='qnorm', gs=64, comb=comb[:, 0:1], dst=lambda qb, h=h: dq_d[h, :, qb * 512:(qb + 1) * 512]))
            for h in range(4):
                specs.append(dict(kind='kraw', gs=64, dst=lambda qb, h=h: dk_d[h, :, qb * 512:(qb + 1) * 512],
                                  rk=lambda qb, h=h: rk_diff[:, qb * 4:(qb + 1) * 4, 2 * h:2 * h + 2]))
            for h in range(4):
                specs.append(dict(kind='silu', dst=lambda qb, h=h: gate_d[4 + h, :, qb * 512:(qb + 1) * 512]))
            for h in range(2):
                specs.append(dict(kind='qnorm', gs=128, comb=comb[:, 1:2], dst=lambda qb, h=h: mq_d[h, :, qb * 512:(qb + 1) * 512]))
            for h in range(2):
                specs.append(dict(kind='silu', dst=lambda qb, h=h: gate_d[8 + h, :, qb * 512:(qb + 1) * 512]))
            for h in range(4):
                specs.append(dict(kind='tm', dt='bf16', ncols=128,
                                  dst=lambda qb, h=h: dv_d[qb * 512:(qb + 1) * 512, h * 128:(h + 1) * 128].rearrange('(s p) n -> p s n', p=128)))
            specs.append(dict(kind='tm', dt='f32', ncols=16,
                              dst=lambda qb: gba_d[qb * 512:(qb + 1) * 512, :].rearrange('(s p) n -> p s n', p=128)))
            phase_proj(P, bb, T, xnT, gcol, wfm_d, specs, cmb)
    if debug:
        bb.dma('sp', rkdbg_d, rk_diff[:].rearrange('p a b -> p (a b)'), [], [])
        P.emit()
    if 'gdn' in phases or 'gdnprep' in phases:
        gkT_d = nc.dram_tensor("gkT", [128, 4, T], BF16, kind=kind_s).ap()
        gqT_d = nc.dram_tensor("gqT", [128, 4, T], BF16, kind=kind_s).ap()
        gkt_d = nc.dram_tensor("gktok", [T, 4, 128], F32, kind=kind_s).ap()
        gvt_d = nc.dram_tensor("gvtok", [T, 4, 128], F32, kind=kind_s).ap()
        gtab = P.sbuf('gtab', [128, NT, 8], F32, persistent=True)
        btab = P.sbuf('btab', [128, NT, 8], F32, persistent=True)
        phase_gdn_prep(P, bb, T, g_raw, gba_d, lambda part, h: cvc('conv%d_%d' % (part, h), 0, 5), gkT_d, gqT_d, gkt_d, gvt_d,
                       gtab, btab, cm, cvc)
        if debug:
            gt_dbg = nc.dram_tensor("gt_dbg", [128, NT * 8], F32, kind=kind_s).ap()
            bt_dbg = nc.dram_tensor("bt_dbg", [128, NT * 8], F32, kind=kind_s).ap()
            bb.dma('sp', gt_dbg, gtab[:].rearrange('p a b -> p (a b)'), [], [])
            bb.dma('sp', bt_dbg, btab[:].rearrange('p a b -> p (a b)'), [], [])
            P.emit()
        if 'gdn' in phases:
            phase_gdn(P, bb, T, gkT_d, gqT_d, gkt_d, gvt_d, gtab, btab, lambda h: gate_d[h],
                      lambda h: y_d[h * 128:(h + 1) * 128, :], cm, cmb, cvc('gdng', 0, 128))
    if 'diff' in phases:
        o, w_ = cvoff['lam']
        ob, _ = cvoff['bfar']
        phase_diff(P, bb, T, dq_d, dk_d, dv_d, lambda h: gate_d[4 + h], rk_diff, toep_d,
                   lambda h: y_d[(4 + h) * 128:(5 + h) * 128, :], cmb, cv[:, o:o + 256], cv[:, ob:ob + 8], cvc('subln'), 0.2)
    if 'mem' in phases:
        og, _ = cvoff['gmem']
        phase_mem(P, bb, T, mem_d, wmem_d, cv[:, og:og + NCH], mq_d, lambda h: gate_d[8 + h],
                  lambda h: y_d[(8 + h) * 128:(9 + h) * 128, :], cmb)
    P.finish()
    return nc


def phase_diff(P, bb, T, dq_d, dk_d, dv_d, gate_rows, rk_diff, toep_d, y_rows, cmb, lamcol, bfar, subln_col, lambda_init):
    NT, NQ = T // 128, T // 512
    with P.phase():
        kT = P.sbuf('d_kT', [128, 4, T], BF16)
        v = P.sbuf('d_v', [128, NT, 512], BF16)
        toep = P.sbuf('d_toep', [128, 4, TOEP_W], F32)
        sc = P.sbuf('d_sc', [128, 8], F32)
        junk = P.sbuf('d_junk', [128, 64], F32)
        for h in range(4):
            bb.dma('sp', kT[:, h, :], dk_d[h], [], ['kT'])
        for t0 in range(0, NT, 8):
            t1 = min(NT, t0 + 8)
            bb.dma('sp', v[:, t0:t1, :], dv_d[t0 * 128:t1 * 128, :].rearrange('(t p) n -> p t n', p=128), [], ['v'])
        bb.dma('sp', toep[:], toep_d, [], ['toep'])
        bb.tt('dve', junk[:], lamcol[:, 0:64], lamcol[:, 64:128], ALU.mult, ['cv'], ['d_junk'])
        P.op('dve', lambda e: e.reduce_sum(out=sc[:, 0:1], in_=junk[:], axis=AX.X), ['d_junk'], ['sc0'])
        bb.tt('dve', junk[:], lamcol[:, 128:192], lamcol[:, 192:256], ALU.mult, ['cv', 'sc0'], ['d_junk'])
        P.op('dve', lambda e: e.reduce_sum(out=sc[:, 1:2], in_=junk[:], axis=AX.X), ['d_junk'], ['sc1'])
        bb.act(sc[:, 2:4], sc[:, 0:2], AF.Exp, ['sc0', 'sc1'], ['sc2'])
        bb.tt('dve', sc[:, 4:5], sc[:, 3:4], sc[:, 2:3], ALU.subtract, ['sc2'], ['sc4'])
        bb.ts('dve', sc[:, 4:5], sc[:, 4:5], -lambda_init, ALU.add, ['sc4'], ['sc4'])
        bb.ts('dve', sc[:, 5:6], subln_col, 1.0 - lambda_init, ALU.mult, ['cv'], ['sc5'])
        qr = Ring([(P.sbuf('d_q%d' % i, [128, 512], BF16), 'd_q%d' % i) for i in range(2)])
        gr = Ring([(P.sbuf('d_g%d' % i, [128, 512], BF16), 'd_g%d' % i) for i in range(2)])
        pr = Ring([(P.sbuf('d_p%d' % i, [128, 512], BF16), 'd_p%d' % i) for i in range(4)])
        tr_ = Ring([(P.sbuf('d_t%d' % i, [128, 512], F32), 'd_t%d' % i) for i in range(3)])
        er = Ring([(P.sbuf('d_e%d' % i, [128, 512], F32), 'd_e%d' % i) for i in range(4)])
        outr = Ring([(P.sbuf('d_o%d' % i, [128, 512], BF16), 'd_o%d' % i) for i in range(2)])
        sqr = Ring([(P.sbuf('d_s%d' % i, [128, 512], BF16), 'd_s%d' % i) for i in range(2)])
        sring = Ring([(P.psum('d_st%d' % i, [128, 512], F32), 'd_st%d' % i) for i in range(4)])
        po = [(P.psum('d_po%d' % i, [128, 512], F32), 'd_po%d' % i) for i in range(2)]
        ps = [(P.psum('d_ps%d' % i, [128, 512], F32), 'd_ps%d' % i) for i in range(2)]
        ones = cmb('ones')
        for qb in range(NQ):
            cs = slice(qb * 512, (qb + 1) * 512)
            for h in range(4):
                q, kq = qr.next()
                g, kg = gr.next()
                bb.dma('sp', q[:], dq_d[h, :, cs], [], [kq])
                bb.dma('sp', g[:], gate_rows(h)[:, cs], [], [kg])
                for c in range(2):
                    rows = slice(64 * c, 64 * c + 64)
                    for kt in range(NT):
                        st, kst = sring.next()
                        bb.mm(st[:], kT[rows, h, kt * 128:(kt + 1) * 128], q[rows, :], ['kT', kq], [kst])
                        d = kt * 128 - qb * 512
                        rk = rk_diff[:, kt, 2 * h + c:2 * h + c + 1]
                        pT, kpt = pr.next()
                        if -128 <= d <= 512:
                            tmp, ktmp = tr_.next()
                            bb.stt('dve', tmp[:], st[:], rk, toep[:, h, 512 - d:1024 - d], ALU.mult, ALU.add,
                                   [kst, 'toep'], [ktmp])
                            bb.act(pT[:], tmp[:], AF.Exp, [ktmp], [kpt])
                        else:
                            sgn = 1 if d > 0 else 0
                            bb.act(pT[:], st[:], AF.Exp, [kst, 'cv'], [kpt], scale=rk, bias=bfar[:, 2 * h + sgn:2 * h + sgn + 1])
                        bb.mm(po[c][0][:], v[:, kt, h * 128:(h + 1) * 128], pT[:], ['v', kpt], [po[c][1]],
                              start=(kt == 0), stop=(kt == NT - 1))
                        bb.mm(ps[c][0][:], ones, pT[:], [kpt], [ps[c][1]], start=(kt == 0), stop=(kt == NT - 1))
                e1, k1 = er.next()
                e2, k2 = er.next()
                bb.recip(e1[:], ps[0][0][:], [ps[0][1]], [k1])
                bb.tt('dve', e1[:], po[0][0][:], e1[:], ALU.mult, [po[0][1], k1], [k1])
                bb.recip(e2[:], ps[1][0][:], [ps[1][1]], [k2])
                bb.tt('dve', e2[:], po[1][0][:], e2[:], ALU.mult, [po[1][1], k2], [k2])
                bb.stt('pool', e1[:], e2[:], sc[:, 4:5], e1[:], ALU.mult, ALU.add, [k1, k2, 'sc4'], [k1])
                sq, ksq = sqr.next()
                bb.act(sq[:], e1[:], AF.Square, [k1], [ksq])
                p2, kp2 = sring.next()
                bb.mm(p2[:], ones, sq[:], [ksq], [kp2])
                bb.ts('dve', e2[:], p2[:], 1.0 / 128, ALU.mult, [kp2], [k2], s2=EPS, op1=ALU.add)
                bb.act(e2[:], e2[:], AF.Sqrt, [k2], [k2])
                bb.recip(e2[:], e2[:], [k2], [k2])
                bb.tt('pool', e1[:], e1[:], e2[:], ALU.mult, [k1, k2], [k1])
                o, ko = outr.next()
                bb.stt('pool', o[:], e1[:], sc[:, 5:6], g[:], ALU.mult, ALU.mult, [k1, kg, 'sc5'], [ko])
                bb.dma('pool', y_rows(h)[:, cs], o[:], [ko], [], final=True)


def phase_mem(P, bb, T, mem_d, wmem_d, gmem, mq_d, gate_rows, y_rows, cmb):
    NQ = T // 512
    with ExitStack() as st0:
        memT = st0.enter_context(P.nc.sbuf_tensor('memT', [128, NCH, 256], BF16))
        mkT = st0.enter_context(P.nc.sbuf_tensor('m_kT', [128, 2, 256], BF16))
        mv = st0.enter_context(P.nc.sbuf_tensor('m_v', [128, 2, 256], BF16))
        rkm = st0.enter_context(P.nc.sbuf_tensor('m_rk', [128, 2, 2], F32))
        phase_xnT(P, bb, 256, lambda tt: mem_d[tt * 128:(tt + 1) * 128, :], memT, cmb('ident'))
        with P.phase():
            ones = cmb('ones')
            wst = Ring([(P.sbuf('mwst%d' % i, [128, NCH, 128], F32), 'mwst%d' % i) for i in range(2)])
            wbr = Ring([(P.sbuf('mwbr%d' % i, [128, NCH, 128], BF16), 'mwbr%d' % i) for i in range(2)])
            pa = Ring([(P.psum('m_pa%d' % i, [128, 512], F32), 'm_pa%d' % i) for i in range(2)])
            sqb = P.sbuf('m_sq', [128, 256], BF16)
            tmpf = P.sbuf('m_tmpf', [128, 8], F32)
            for i in range(4):
                wt, kw = wst.next()
                wbt, kwb = wbr.next()
                bb.dma('sp', wt[:], wmem_d[i].rearrange('p (c n) -> p c n', c=NCH), [], [kw])
                bb.tt('dve', wbt[:], wt[:], gmem.unsqueeze(2).to_broadcast([128, NCH, 128]), ALU.mult, [kw], [kwb])
                ps, kp = pa.next()
                h = i % 2
                if i < 2:
                    for c in range(NCH):
                        bb.mm(ps[:, 0:256], wbt[:, c, :], memT[:, c, :], [kwb], [kp], start=(c == 0), stop=(c == NCH - 1))
                    bb.copy('act', mkT[:, h, :], ps[:, 0:256], [kp], ['mkT'])
                    bb.act(sqb[:], ps[:, 0:256], AF.Square, [kp], ['m_sq'])
                    p2, kp2 = pa.next()
                    for s in range(2):
                        bb.mm(p2[:, s:s + 1], sqb[:, s * 128:(s + 1) * 128], ones[:, 0:1], ['m_sq'], [kp2])
                    bb.ts('dve', tmpf[:, 0:2], p2[:, 0:2], 1.0 / 128, ALU.mult, [kp2], ['m_tmpf'], s2=EPS, op1=ALU.add)
                    bb.act(tmpf[:, 0:2], tmpf[:, 0:2], AF.Sqrt, ['m_tmpf'], ['m_tmpf'])
                    bb.recip(rkm[:, :, h], tmpf[:, 0:2], ['m_tmpf'], ['rkm'])
                else:
                    for s in range(2):
                        for c in range(NCH):
                            bb.mm(ps[:, s * 128:(s + 1) * 128], memT[:, c, s * 128:(s + 1) * 128], wbt[:, c, :], [kwb], [kp],
                                  start=(c == 0), stop=(c == NCH - 1))
                    bb.copy('act', mv[:, :, h * 128:(h + 1) * 128], ps[:, 0:256].rearrange('p (s n) -> p s n', s=2), [kp], ['mv'])
        with P.phase():
            ones = cmb('ones')
            qr = Ring([(P.sbuf('m_q%d' % i, [128, 512], BF16), 'm_q%d' % i) for i in range(2)])
            gr = Ring([(P.sbuf('m_g%d' % i, [128, 512], BF16), 'm_g%d' % i) for i in range(2)])
            pr = Ring([(P.sbuf('m_p%d' % i, [128, 512], BF16), 'm_p%d' % i) for i in range(3)])
            er = Ring([(P.sbuf('m_e%d' % i, [128, 512], F32), 'm_e%d' % i) for i in range(2)])
            outr = Ring([(P.sbuf('m_o%d' % i, [128, 512], BF16), 'm_o%d' % i) for i in range(2)])
            sring = Ring([(P.psum('m_st%d' % i, [128, 512], F32), 'm_st%d' % i) for i in range(3)])
            por = Ring([(P.psum('m_po%d' % i, [128, 512], F32), 'm_po%d' % i) for i in range(2)])
            psr = Ring([(P.psum('m_ps%d' % i, [128, 512], F32), 'm_ps%d' % i) for i in range(2)])
            for qb in range(NQ):
                cs = slice(qb * 512, (qb + 1) * 512)
                for h in range(2):
                    q, kq = qr.next()
                    g, kg = gr.next()
                    bb.dma('sp', q[:], mq_d[h, :, cs], [], [kq])
                    bb.dma('sp', g[:], gate_rows(h)[:, cs], [], [kg])
                    po, kpo = por.next()
                    ps, kps = psr.next()
                    for mt in range(2):
                        st, kst = sring.next()
                        bb.mm(st[:], mkT[:, h, mt * 128:(mt + 1) * 128], q[:], [kq], [kst])
                        pT, kpt = pr.next()
                        bb.act(pT[:], st[:], AF.Exp, [kst], [kpt], scale=rkm[:, mt, h:h + 1])
                        bb.mm(po[:], mv[:, mt, h * 128:(h + 1) * 128], pT[:], [kpt], [kpo], start=(mt == 0), stop=(mt == 1))
                        bb.mm(ps[:], ones, pT[:], [kpt], [kps], start=(mt == 0), stop=(mt == 1))
                    e, ke = er.next()
                    bb.recip(e[:], ps[:], [kps], [ke])
                    bb.tt('dve', e[:], po[:], e[:], ALU.mult, [kpo, ke], [ke])
                    o, ko = outr.next()
                    bb.tt('pool', o[:], e[:], g[:], ALU.mult, [ke, kg], [ko])
                    bb.dma('pool', y_rows(h)[:, cs], o[:], [ko], [], final=True)


def phase_gdn_prep(P, bb, T, g_raw, gba_d, convcol, kT_d, qT_d, ktok_d, vtok_d, gtab, btab, cm, cvc):
    NT, NQ = T // 128, T // 512
    with P.phase():
        z = P.sbuf('gz', [128, 2], F32)
        bb.memset('pool', z[:], 0.0, ['gz'])
        for ft in range(12):
            bb.dma('sp', g_raw[ft, :, 0:2], z[:], ['gz'], ['halo'])
            bb.dma('sp', g_raw[ft, :, T + 2:T + 4], z[:], ['gz'], ['halo'])
        gba = P.sbuf('gba_sb', [128, NT, 16], F32)
        tmp = P.sbuf('gtmp', [128, NT, 8], F32)
        nea = P.sbuf('gnea', [128, 8], F32)
        for t0 in range(0, NT, 8):
            t1 = min(NT, t0 + 8)
            bb.dma('sp', gba[:, t0:t1, :], gba_d[t0 * 128:t1 * 128, :].rearrange('(t p) n -> p t n', p=128), [], ['gba'])
        bb.act(nea[:], cvc('alog', 0, 8), AF.Exp, ['cv'], ['nea'])
        bb.ts('dve', nea[:], nea[:], -1.0, ALU.mult, ['nea'], ['nea'])
        bb.tt('dve', tmp[:], gba[:, :, 8:16], cvc('dtb', 0, 8).unsqueeze(1).to_broadcast([128, NT, 8]), ALU.add, ['gba', 'cv'], ['gtmp'])
        bb.act(tmp[:], tmp[:], AF.Exp, ['gtmp'], ['gtmp'])
        bb.ts('dve', tmp[:], tmp[:], 1.0, ALU.add, ['gtmp'], ['gtmp'])
        bb.act(tmp[:], tmp[:], AF.Ln, ['gtmp'], ['gtmp'])
        bb.tt('dve', gtab[:], tmp[:], nea[:].unsqueeze(1).to_broadcast([128, NT, 8]), ALU.mult, ['gtmp', 'nea'], ['gtab'])
        bb.act(btab[:], gba[:, :, 0:8], AF.Exp, ['gba'], ['btab'], scale=-1.0)
        bb.ts('dve', btab[:], btab[:], 1.0, ALU.add, ['btab'], ['btab'])
        bb.recip(btab[:], btab[:], ['btab'], ['btab'])
        xin = Ring([(P.sbuf('gx%d' % i, [128, 516], F32), 'gx%d' % i) for i in range(3)])
        acc = Ring([(P.sbuf('gacc%d' % i, [128, 512], F32), 'gacc%d' % i) for i in range(2)])
        sil = Ring([(P.sbuf('gsil%d' % i, [128, 512], F32), 'gsil%d' % i) for i in range(2)])
        sqf = Ring([(P.sbuf('gsq%d' % i, [128, 512], F32), 'gsq%d' % i) for i in range(2)])
        tok = Ring([(P.sbuf('gtok%d' % i, [128, 4, 128], F32), 'gtok%d' % i) for i in range(3)])
        fmb = Ring([(P.sbuf('gfm%d' % i, [128, 512], BF16), 'gfm%d' % i) for i in range(2)])
        ssr = Ring([(P.sbuf('gss%d' % i, [128, 4], F32), 'gss%d' % i) for i in range(3)])
        pp = Ring([(P.psum('gpp%d' % i, [128, 512], F32), 'gpp%d' % i) for i in range(6)])
        identf = cm('ident')
        for qb in range(NQ):
            for ft in range(12):
                part, h = ft // 4, ft % 4
                x, kx = xin.next()
                bb.dma('sp', x[:], g_raw[ft, :, qb * 512:qb * 512 + 516], ['halo'], [kx])
                a, ka = acc.next()
                w = convcol(part, h)
                bb.ts('pool', a[:], x[:, 0:512], w[:, 0:1], ALU.mult, [kx, 'cv'], [ka])
                for j in range(1, 5):
                    bb.stt('dve', a[:], x[:, j:j + 512], w[:, j:j + 1], a[:], ALU.mult, ALU.add, [kx, ka], [ka])
                s, ks = sil.next()
                bb.act(s[:], a[:], AF.Silu, [ka], [ks])
                pt, kpt = pp.next()
                for sub in range(4):
                    bb.tr(pt[:, sub * 128:(sub + 1) * 128], s[:, sub * 128:(sub + 1) * 128], identf, [ks], [kpt])
                tk, ktk = tok.next()
                rows = slice(qb * 512, (qb + 1) * 512)
                if part == 2:
                    bb.copy('act', tk[:], pt[:].rearrange('p (s f) -> p s f', s=4), [kpt], [ktk])
                    bb.dma('pool', vtok_d[rows, h, :].rearrange('(s p) f -> p s f', p=128), tk[:], [ktk], [])
                    continue
                sq, ksq = sqf.next()
                bb.act(sq[:], pt[:], AF.Square, [kpt], [ksq])
                ss, kss = ssr.next()
                P.op('dve', lambda e, ss=ss, sq=sq: e.reduce_sum(out=ss[:], in_=sq[:].rearrange('p (s f) -> p s f', s=4), axis=AX.X),
                     [ksq], [kss])
                bb.ts('dve', ss[:], ss[:], EPS, ALU.add, [kss], [kss])
                bb.act(ss[:], ss[:], AF.Sqrt, [kss], [kss])
                bb.recip(ss[:], ss[:], [kss], [kss])
                if part == 0:
                    bb.ts('dve', ss[:], ss[:], 128 ** -0.5, ALU.mult, [kss], [kss])
                bb.tt('dve', tk[:], pt[:].rearrange('p (s f) -> p s f', s=4), ss[:].unsqueeze(2).to_broadcast([128, 4, 128]),
                      ALU.mult, [kpt, kss], [ktk])
                if part == 1:
                    bb.dma('pool', ktok_d[rows, h, :].rearrange('(s p) f -> p s f', p=128), tk[:], [ktk], [])
                p2, kp2 = pp.next()
                for sub in range(4):
                    bb.tr(p2[:, sub * 128:(sub + 1) * 128], tk[:, sub, :], identf, [ktk], [kp2])
                fm, kfm = fmb.next()
                bb.copy('act', fm[:], p2[:], [kp2], [kfm])
                bb.dma('pool', (qT_d if part == 0 else kT_d)[:, h, qb * 512:(qb + 1) * 512], fm[:], [kfm], [])


def phase_gdn(P, bb, T, kT_d, qT_d, ktok_d, vtok_d, gtab, btab, gate_rows, y_rows, cm, cmb, gdng):
    NC = T // 128
    with P.phase():
        identf = cm('ident')
        identb, onesb, negonesb = cmb('ident'), cmb('ones'), cmb('negones')
        tri = [cm('tri_f'), cm('tri_b')]
        trib = [cmb('tri_f'), cmb('tri_b')]
        tricb = [cmb('tric_f'), cmb('tric_b')]
        negs = [cmb('negs_f'), cmb('negs_b')]
        negi = [cmb('negi_f'), cmb('negi_b')]
        o_acc = P.sbuf('o_acc', [128, NC, 4, 128], F32)
        S = [[P.sbuf('S%d%d' % (d, h), [128, 128], F32) for h in range(4)] for d in range(2)]
        Sb = [[P.sbuf('Sb%d%d' % (d, h), [128, 128], BF16) for h in range(4)] for d in range(2)]
        for d in range(2):
            for h in range(4):
                bb.memset('pool', S[d][h][:], 0.0, [('S', d, h)])
                bb.memset('pool', Sb[d][h][:], 0.0, [('Sb', d, h)])
        banks = [P.psum('gps%d' % i, [128, 512], F32) for i in range(6)]
        slots = Ring([(banks[i][:, 0:128], 'gslot%d' % i) for i in range(6)])
        tbanks = [P.psum('gpst%d' % i, [128, 1024], BF16) for i in range(2)]
        tslots = Ring([(tbanks[i][:, 0:128], 'gtslot%d' % i) for i in range(2)])

        def ring(name, n, shape, dt):
            return Ring([(P.sbuf('%s%d' % (name, i), shape, dt), '%s%d' % (name, i)) for i in range(n)])
        r_kT = ring('r_kT', 3, [128, 4, 128], BF16)
        r_qT = ring('r_qT', 3, [128, 4, 128], BF16)
        r_kt = ring('r_kt', 3, [128, 4, 128], F32)
        r_vt = ring('r_vt', 3, [128, 4, 128], F32)
        r_gs = ring('r_gs', 3, [128, 16], F32)
        r_bg = ring('r_bg', 3, [128, 8], F32)
        r_sp = ring('r_sp', 3, [128, 12], BF16)
        r_sr = ring('r_sr', 3, [128, 8], F32)
        r_f = ring('r_f', 16, [128, 128], F32)
        r_b = ring('r_b', 64, [128, 128], BF16)
        r_g = ring('r_g', 3, [128, 128], BF16)
        r_s4 = ring('r_s4', 4, [128, 4], F32)
        for s in range(NC):
            for d in range(2):
                c = s if d == 0 else NC - 1 - s
                cs = slice(c * 128, (c + 1) * 128)
                kT, kkT = r_kT.next()
                qT, kqT = r_qT.next()
                kt, kkt = r_kt.next()
                vt, kvt = r_vt.next()
                bb.dma('sp', kT[:], kT_d[:, :, cs], [], [kkT])
                bb.dma('sp', qT[:], qT_d[:, :, cs], [], [kqT])
                bb.dma('sp', kt[:], ktok_d[cs, :, :], [], [kkt])
                bb.dma('sp', vt[:], vtok_d[cs, :, :], [], [kvt])
                gcols = gtab[:, c, d * 4:(d + 1) * 4]
                bcols = btab[:, c, d * 4:(d + 1) * 4]
                sp, ksp = r_sp.next()
                sr, ksr = r_sr.next()
                bb.copy('dve', sp[:, 0:4], gcols, [], [ksp])
                bb.tt('dve', sr[:, 0:4], gcols, sp[:, 0:4], ALU.subtract, [ksp], [ksr])
                bb.copy('dve', sp[:, 4:8], sr[:, 0:4], [ksr], [ksp])
                bb.tt('dve', sr[:, 4:8], sr[:, 0:4], sp[:, 4:8], ALU.subtract, [ksp, ksr], [ksr])
                bb.copy('dve', sp[:, 8:12], sr[:, 4:8], [ksr], [ksp])
                pg, kpg = slots.next()
                for j in range(3):
                    bb.mm(pg[:, 0:4], trib[d], sp[:, 4 * j:4 * j + 4], [ksp], [kpg], start=(j == 0), stop=(j == 2))
                for j in range(3):
                    bb.mm(pg[:, 4:8], tricb[d], sp[:, 4 * j:4 * j + 4], [ksp], [kpg], start=(j == 0), stop=(j == 2))
                gs, kgs = r_gs.next()
                bg, kbg = r_bg.next()
                bb.copy('act', gs[:, 0:4], pg[:, 0:4], [kpg], [kgs])
                bb.ts('dve', gs[:, 4:8], pg[:, 0:4], -1.0, ALU.mult, [kpg], [kgs])
                bb.act(gs[:, 8:16], pg[:, 0:8], AF.Exp, [kpg], [kgs])
                bb.ts('dve', bg[:, 0:4], bcols, -1.0, ALU.mult, [], [kbg])
                bb.tt('dve', bg[:, 4:8], bcols, gs[:, 8:12], ALU.mult, [kgs], [kbg])
                second = (2 * s >= NC)
                for h in range(4):
                    gT = []
                    for j in range(3):
                        t_, kt_ = r_b.next()
                        bb.ts('pool', t_[:], tri[d], sp[:, 4 * j + h:4 * j + h + 1], ALU.mult, [ksp], [kt_])
                        gT.append((t_, kt_))
                    pa, kpa = slots.next()
                    for j in range(3):
                        bb.mm(pa, onesb, gT[j][0][:], [gT[j][1]], [kpa], start=(j == 0), stop=(j == 2))
                    eG, keG = r_f.next()
                    bb.act(eG[:], pa, AF.Exp, [kpa], [keG])
                    pb, kpb = slots.next()
                    for j in range(3):
                        bb.mm(pb, onesb, gT[j][0][:], [gT[j][1]], [kpb], start=(j == 0), stop=False)
                    bb.mm(pb, identb, negi[d], [], [kpb], start=False, stop=True)
                    DiT, kDiT = r_f.next()
                    bb.act(DiT[:], pb, AF.Exp, [kpb, kgs], [kDiT], bias=gs[:, 4 + h:5 + h])
                    pc, kpc = slots.next()
                    for j in range(3):
                        bb.mm(pc, negonesb, gT[j][0][:], [gT[j][1]], [kpc], start=(j == 0), stop=False)
                    bb.mm(pc, identb, negs[d], [], [kpc], start=False, stop=True)
                    Ds, kDs = r_f.next()
                    bb.act(Ds[:], pc, AF.Exp, [kpc, kgs], [kDs], bias=gs[:, h:h + 1])
                    pkk, kpkk = slots.next()
                    bb.mm(pkk, kT[:, h, :], kT[:, h, :], [kkT], [kpkk])
                    pqk, kpqk = slots.next()
                    bb.mm(pqk, kT[:, h, :], qT[:, h, :], [kkT, kqT], [kpqk])
                    N0f, kN0f = r_f.next()
                    bb.stt('dve', N0f[:], pkk, bg[:, h:h + 1], Ds[:], ALU.mult, ALU.mult, [kpkk, kbg, kDs], [kN0f])
                    N, kN = r_b.next()
                    bb.copy('act', N[:], N0f[:], [kN0f], [kN])
                    N0l, kN0l = r_b.next()
                    bb.tt('pool', N0l[:], N0f[:], N[:], ALU.subtract, [kN0f, kN], [kN0l])
                    AT, kAT = r_b.next()
                    bb.tt('dve', AT[:], pqk, DiT[:], ALU.mult, [kpqk, kDiT], [kAT])
                    pp0, kpp0 = tslots.next()
                    bb.tr(pp0, N[:], identb, [kN], [kpp0])
                    Pm, kPm = r_b.next()
                    bb.copy('act', Pm[:], pp0, [kpp0], [kPm])
                    TT, kTT = r_b.next()
                    bb.tt('dve', TT[:], pp0, identb, ALU.add, [kpp0], [kTT])
                    P0h, kP0h = Pm, kPm
                    ppl, kppl = tslots.next()
                    bb.tr(ppl, N0l[:], identb, [kN0l], [kppl])
                    P0l, kP0l = r_b.next()
                    bb.copy('act', P0l[:], ppl, [kppl], [kP0l])
                    for k in range(1, 7):
                        pn, kpn = slots.next()
                        bb.mm(pn, Pm[:], N[:], [kPm, kN], [kpn])
                        N2, kN2 = r_b.next()
                        bb.copy('act', N2[:], pn, [kpn], [kN2])
                        if k < 6:
                            ppk, kppk = slots.next()
                            bb.mm(ppk, N[:], Pm[:], [kPm, kN], [kppk])
                            P2, kP2 = r_b.next()
                            bb.copy('act' if k % 2 else 'dve', P2[:], ppk, [kppk], [kP2])
                        ptt, kptt = slots.next()
                        bb.mm(ptt, N2[:], TT[:], [kN2, kTT], [kptt])
                        TT2, kTT2 = r_b.next()
                        bb.tt('dve', TT2[:], ptt, TT[:], ALU.add, [kptt, kTT], [kTT2])
                        N, kN, TT, kTT = N2, kN2, TT2, kTT2
                        if k < 6:
                            Pm, kPm = P2, kP2
                    ptr, kptr = tslots.next()
                    bb.tr(ptr, TT[:], identb, [kTT], [kptr])
                    Tt, kTt = r_b.next()
                    bb.copy('act', Tt[:], ptr, [kptr], [kTt])
                    ImT, kImT = r_f.next()
                    bb.tt('dve', ImT[:], identb, ptr, ALU.subtract, [kptr], [kImT])
                    pe_, kpe = slots.next()
                    bb.mm(pe_, P0h[:], Tt[:], [kP0h, kTt], [kpe], start=True, stop=False)
                    bb.mm(pe_, P0l[:], Tt[:], [kP0l, kTt], [kpe], start=False, stop=True)
                    Ep, kEp = r_b.next()
                    bb.tt('dve', Ep[:], pe_, ImT[:], ALU.add, [kpe, kImT], [kEp])
                    pr_, kpr = slots.next()
                    bb.mm(pr_, Ep[:], TT[:], [kEp, kTT], [kpr])
                    TTr, kTTr = r_b.next()
                    bb.tt('dve', TTr[:], pr_, TT[:], ALU.add, [kpr, kTT], [kTTr])
                    TT, kTT = TTr, kTTr
                    vb, kvb = r_b.next()
                    bb.ts('pool', vb[:], vt[:, h, :], bcols[:, h:h + 1], ALU.mult, [kvt], [kvb])
                    kbgm, kkbgm = r_b.next()
                    bb.ts('pool', kbgm[:], kt[:, h, :], bg[:, 4 + h:5 + h], ALU.mult, [kkt, kbg], [kkbgm])
                    kdec, kkdec = r_b.next()
                    bb.ts('pool', kdec[:], kt[:, h, :], gs[:, 12 + h:13 + h], ALU.mult, [kkt, kgs], [kkdec])
                    qd, kqd = r_b.next()
                    bb.tt('pool', qd[:], qT[:, h, :], eG[:], ALU.mult, [kqT, keG], [kqd])
                    pu, kpu = slots.next()
                    bb.mm(pu, TT[:], vb[:], [kTT, kvb], [kpu])
                    u, ku = r_f.next()
                    bb.copy('act', u[:], pu, [kpu], [ku])
                    pw, kpw = slots.next()
                    bb.mm(pw, kbgm[:], TT[:], [kkbgm, kTT], [kpw])
                    wT, kwT = r_b.next()
                    bb.copy('dve', wT[:], pw, [kpw], [kwT])
                    kS, kSb = ('S', d, h), ('Sb', d, h)
                    pws, kpws = slots.next()
                    bb.mm(pws, wT[:], Sb[d][h][:], [kwT, kSb], [kpws])
                    vn, kvn = r_b.next()
                    bb.tt('dve', vn[:], u[:], pws, ALU.subtract, [ku, kpws], [kvn])
                    po, kpo = slots.next()
                    bb.mm(po, qd[:], Sb[d][h][:], [kqd, kSb], [kpo], start=True, stop=False)
                    bb.mm(po, AT[:], vn[:], [kAT, kvn], [kpo], start=False, stop=True)
                    pds, kpds = slots.next()
                    bb.mm(pds, kdec[:], vn[:], [kkdec, kvn], [kpds])
                    last = 127 if d == 0 else 0
                    bb.stt('dve', S[d][h][:], S[d][h][:], eG[:, last:last + 1], pds, ALU.mult, ALU.add, [kS, keG, kpds], [kS])
                    bb.copy('act', Sb[d][h][:], S[d][h][:], [kS], [kSb])
                    ko = ('oacc', c, h)
                    if not second:
                        bb.copy('act', o_acc[:, c, h, :], po, [kpo], [ko])
                        continue
                    bb.tt('dve', o_acc[:, c, h, :], po, o_acc[:, c, h, :], ALU.add, [kpo, ko], [ko])
                    s4, ks4 = r_s4.next()
                    jk, kjk = r_f.next()
                    bb.act(jk[:], o_acc[:, c, h, :], AF.Square, [ko], [kjk, ks4], accum=s4[:, 0:1])
                    bb.ts('dve', s4[:, 1:2], s4[:, 0:1], 1.0 / 128, ALU.mult, [ks4], [ks4], s2=EPS, op1=ALU.add)
                    bb.act(s4[:, 1:2], s4[:, 1:2], AF.Sqrt, [ks4], [ks4])
                    bb.recip(s4[:, 2:3], s4[:, 1:2], [ks4], [ks4])
                    on, kon = r_b.next()
                    bb.stt('dve', on[:], o_acc[:, c, h, :], s4[:, 2:3], gdng, ALU.mult, ALU.mult, [ko, ks4], [kon])
                    pT_, kpT = tslots.next()
                    bb.tr(pT_, on[:], identb, [kon], [kpT])
                    g, kg = r_g.next()
                    bb.dma('sp', g[:], gate_rows(h)[:, cs], [], [kg])
                    yo, kyo = r_b.next()
                    bb.tt('dve', yo[:], pT_, g[:], ALU.mult, [kpT, kg], [kyo])
                    bb.dma('pool', y_rows(h)[:, cs], yo[:], [kyo], [], final=True)


def phase_wout(P, bb, T0, T1, yT_d, wo_d, x_rows, out_rows, final):
    NF = 20
    with P.phase():
        wo = P.sbuf('wo', [128, NF, D], BF16)
        wst = Ring([(P.sbuf('wost%d' % i, [128, D], F32), 'wost%d' % i) for i in range(2)])
        for c in range(NF):
            wt, kw = wst.next()
            bb.dma('sp', wt[:], wo_d[:, c, :], [], [kw])
            bb.copy(('dve', 'pool', 'act')[c % 3], wo[:, c, :], wt[:], [kw], [('wo', c)])
        yr = Ring([(P.sbuf('wy%d' % i, [128, NF, 512], BF16), 'wy%d' % i) for i in range(2)])
        xr = Ring([(P.sbuf('wx%d' % i, [128, D], F32), 'wx%d' % i) for i in range(2)])
        orr = Ring([(P.sbuf('wor%d' % i, [128, D], F32), 'wor%d' % i) for i in range(2)])
        pr = Ring([(P.psum('wps%d' % i, [128, 512], F32), 'wps%d' % i) for i in range(6)])
        wkeys = [('wo', c) for c in range(NF)]
        for qb in range(T0 // 512, T1 // 512):
            yb, ky = yr.next()
            for part in range(4):
                bb.dma('sp', yb[:, part * 5:(part + 1) * 5, :],
                       yT_d[part * 640:(part + 1) * 640, qb * 512:(qb + 1) * 512].rearrange('(c p) t -> p c t', p=128), [], [ky])
            for s in range(4):
                tt = qb * 4 + s
                xt, kx = xr.next()
                ot, ko = orr.next()
                bb.dma('sp', xt[:], x_rows(tt), [], [kx])
                for nb in range(4):
                    ps, kp = pr.next()
                    for c in range(NF):
                        bb.mm(ps[:], yb[:, c, s * 128:(s + 1) * 128], wo[:, c, nb * 512:(nb + 1) * 512], [ky] + (wkeys if c == 0 else []),
                              [kp], start=(c == 0), stop=(c == NF - 1))
                    bb.tt('dve', ot[:, nb * 512:(nb + 1) * 512], ps[:], xt[:, nb * 512:(nb + 1) * 512], ALU.add, [kp, kx], [ko])
                bb.dma('pool', out_rows(tt), ot[:], [ko], [], final=final)


def phase_swa(P, bb, T, sq_d, sk_d, sv_d, rk_s, toep_d, gate_rows, y_rows, cmb, sinkcol):
    NT, NQ = T // 128, T // 512
    with P.phase():
        ones = cmb('ones')
        kT = P.sbuf('s_kT', [128, T], BF16)
        v = P.sbuf('s_v', [128, NT, 128], BF16)
        toep = P.sbuf('s_toep', [128, 3, 4, 128], F32)
        esink = P.sbuf('s_esink', [128, 4], F32)
        bb.dma('sp', kT[:], sk_d, [], ['kT'])
        for t0 in range(0, NT, 8):
            t1 = min(NT, t0 + 8)
            bb.dma('sp', v[:, t0:t1, :], sv_d[t0 * 128:t1 * 128, :].rearrange('(t p) n -> p t n', p=128), [], ['v'])
        bb.dma('sp', toep[:], toep_d, [], ['toep'])
        bb.act(esink[:], sinkcol, AF.Exp, ['cv'], ['esink'])
        qr = Ring([(P.sbuf('s_q%d' % i, [128, 4, 512], BF16), 's_q%d' % i) for i in range(2)])
        gr = Ring([(P.sbuf('s_g%d' % i, [128, 4, 512], BF16), 's_g%d' % i) for i in range(2)])
        outr = Ring([(P.sbuf('s_o%d' % i, [128, 4, 512], BF16), 's_o%d' % i) for i in range(2)])
        tr_ = Ring([(P.sbuf('s_t%d' % i, [128, 512], F32), 's_t%d' % i) for i in range(3)])
        pr = Ring([(P.sbuf('s_p%d' % i, [128, 512], BF16), 's_p%d' % i) for i in range(4)])
        er = Ring([(P.sbuf('s_e%d' % i, [128, 512], F32), 's_e%d' % i) for i in range(2)])
        sring = Ring([(P.psum('s_st%d' % i, [128, 512], F32), 's_st%d' % i) for i in range(4)])
        por = Ring([(P.psum('s_po%d' % i, [128, 512], F32), 's_po%d' % i) for i in range(2)])
        psr = Ring([(P.psum('s_ps%d' % i, [128, 512], F32), 's_ps%d' % i) for i in range(2)])
        for qb in range(NQ):
            cs = slice(qb * 512, (qb + 1) * 512)
            q, kq = qr.next()
            g, kg = gr.next()
            o, ko = outr.next()
            for h in range(4):
                bb.dma('sp', q[:, h, :], sq_d[h, :, cs], [], [kq])
                bb.dma('sp', g[:, h, :], gate_rows(h)[:, cs], [], [kg])
            for s in range(4):
                qt = qb * 4 + s
                offs = [off for off in (-1, 0, 1) if 0 <= qt + off < NT]
                po, kpo = por.next()
                ps, kps = psr.next()
                for j, off in enumerate(offs):
                    kt = qt + off
                    st, kst = sring.next()
                    bb.mm(st[:].rearrange('p (h t) -> p h t', h=4), kT[:, kt * 128:(kt + 1) * 128], q[:, :, s * 128:(s + 1) * 128],
                          ['kT', kq], [kst])
                    tmp, ktmp = tr_.next()
                    bb.stt('dve', tmp[:], st[:], rk_s[:, kt, 0:1], toep[:, off + 1, :, :].rearrange('p h t -> p (h t)'), ALU.mult, ALU.add,
                           [kst, 'toep'], [ktmp])
                    pT, kpt = pr.next()
                    bb.act(pT[:], tmp[:], AF.Exp, [ktmp], [kpt])
                    bb.mm(po[:], v[:, kt, :], pT[:], ['v', kpt], [kpo], start=(j == 0), stop=(j == len(offs) - 1))
                    bb.mm(ps[:], ones, pT[:], [kpt], [kps], start=(j == 0), stop=(j == len(offs) - 1))
                e, ke = er.next()
                bb.tt('dve', e[:].rearrange('p (h t) -> p h t', h=4), ps[:].rearrange('p (h t) -> p h t', h=4),
                      esink[:].unsqueeze(2).to_broadcast([128, 4, 128]), ALU.add, [kps, 'esink'], [ke])
                bb.recip(e[:], e[:], [ke], [ke])
                bb.tt('dve', e[:], po[:], e[:], ALU.mult, [kpo, ke], [ke])
                bb.tt('pool', o[:, :, s * 128:(s + 1) * 128], e[:].rearrange('p (h t) -> p h t', h=4), g[:, :, s * 128:(s + 1) * 128], ALU.mult,
                      [ke, kg], [ko])
            for h in range(4):
                bb.dma('pool', y_rows(h)[:, cs], o[:, h, :], [ko], [], final=True)


import math
_TWOPI = 2.0 * math.pi
CW1 = 6.28125
_r1 = _TWOPI - CW1
CW2 = float((np.float32(_r1).view(np.uint32) & np.uint32(0xFFFFF000)).view(np.float32))
CW3 = float(np.float32(_r1 - CW2))


def phase_mla(P, bb, T, mlqn_d, mlqr_d, ckv_d, kr_d, wup_d, pos_d, gate_rows, y_rows, cm, cmb, invf, kgr):
    NT, NQ = T // 128, T // 512
    nc = P.nc
    with ExitStack() as st0:
        def pers(name, shape, dt):
            return st0.enter_context(nc.sbuf_tensor(name, shape, dt))
        sinT = pers('ml_sin', [128, T], F32)
        cosT = pers('ml_cos', [128, T], F32)
        kn = pers('ml_kn', [128, 4, T], BF16)
        vsb = pers('ml_v', [128, NT, 512], BF16)
        krot = pers('ml_krot', [128, T], BF16)
        rkm = pers('ml_rk', [128, NT, 4], F32)
        ones, rot = cmb('ones'), cmb('rot')
        with P.phase():
            posi = P.sbuf('ml_posi', [128, T], I32)
            ang = P.sbuf('ml_ang', [128, T], F32)
            kk = P.sbuf('ml_kk', [128, T], F32)
            bb.dma('sp', posi[:], pos_d, [], ['posi'])
            bb.copy('dve', ang[:], posi[:], ['posi'], ['ang'])
            bb.ts('dve', ang[:], ang[:], invf, ALU.mult, ['ang', 'cv'], ['ang'])
            bb.ts('dve', kk[:], ang[:], 1.0 / _TWOPI, ALU.mult, ['ang'], ['kk'])
            bb.ts('dve', kk[:], kk[:], 12582912.0, ALU.add, ['kk'], ['kk'], s2=12582912.0, op1=ALU.subtract)
            for cw in (CW1, CW2, CW3):
                bb.stt('dve', ang[:], kk[:], -cw, ang[:], ALU.mult, ALU.add, ['kk', 'ang'], ['ang'])
            bb.ts('dve', ang[:], ang[:], 3.1415925, ALU.min, ['ang'], ['ang'], s2=-3.1415925, op1=ALU.max)
            bb.act(sinT[:], ang[:], AF.Sin, ['ang'], ['sinT'])
            bb.ts('dve', kk[:], ang[:], -1.0, ALU.mult, ['ang'], ['kk'])
            bb.tt('dve', kk[:], kk[:], ang[:], ALU.max, ['ang', 'kk'], ['kk'])
            bb.ts('dve', kk[:], kk[:], -1.0, ALU.mult, ['kk'], ['kk'], s2=math.pi / 2, op1=ALU.add)
            bb.act(cosT[:], kk[:], AF.Sin, ['kk'], ['cosT'])
        with P.phase():
            wst = Ring([(P.sbuf('ml_wst%d' % i, [128, 512], F32), 'ml_wst%d' % i) for i in range(2)])
            wupn = P.sbuf('ml_wupn', [128, 4, 4, 128], BF16)
            wupv = P.sbuf('ml_wupv', [128, 4, 4, 128], BF16)
            for i in range(8):
                wt, kw = wst.next()
                bb.dma('sp', wt[:], wup_d[i], [], [kw])
                if i < 4:
                    bb.copy('dve', wupn[:, i, :, :], wt[:].rearrange('p (c n) -> p c n', c=4), [kw], ['wupn'])
                else:
                    bb.copy('dve', wupv[:, :, i - 4, :], wt[:].rearrange('p (c n) -> p c n', c=4), [kw], ['wupv'])
            ckr = Ring([(P.sbuf('ml_ck%d' % i, [128, 4, 512], BF16), 'ml_ck%d' % i) for i in range(2)])
            krr = Ring([(P.sbuf('ml_kr%d' % i, [128, 512], F32), 'ml_kr%d' % i) for i in range(2)])
            sqr = Ring([(P.sbuf('ml_sq%d' % i, [128, 512], BF16), 'ml_sq%d' % i) for i in range(3)])
            fr = Ring([(P.sbuf('ml_f%d' % i, [128, 512], F32), 'ml_f%d' % i) for i in range(3)])
            br = Ring([(P.sbuf('ml_b%d' % i, [128, 512], BF16), 'ml_b%d' % i) for i in range(2)])
            smr = Ring([(P.sbuf('ml_sm%d' % i, [128, 32], F32), 'ml_sm%d' % i) for i in range(2)])
            pa = Ring([(P.psum('ml_pa%d' % i, [128, 512], F32), 'ml_pa%d' % i) for i in range(5)])
            p2r = Ring([(P.psum('ml_p2%d' % i, [128, 512], F32), 'ml_p2%d' % i) for i in range(2)])
            for qb in range(NQ):
                cs = slice(qb * 512, (qb + 1) * 512)
                ck, kck = ckr.next()
                for c in range(4):
                    bb.dma('sp', ck[:, c, :], ckv_d[c, :, cs], [], [kck])
                kr, kkr = krr.next()
                bb.dma('sp', kr[:], kr_d[:, cs], [], [kkr])
                p2, kp2 = p2r.next()
                for h in range(4):
                    ps, kp = pa.next()
                    for c in range(4):
                        bb.mm(ps[:], wupn[:, h, c, :], ck[:, c, :], ['wupn', kck], [kp], start=(c == 0), stop=(c == 3))
                    bb.copy('act', kn[:, h, cs], ps[:], [kp], [])
                    sq, ksq = sqr.next()
                    bb.act(sq[:], ps[:], AF.Square, [kp], [ksq])
                    for s in range(4):
                        bb.mm(p2[:, s * 4 + h:s * 4 + h + 1], sq[:, s * 128:(s + 1) * 128], ones[:, 0:1], [ksq], [kp2])
                sq, ksq = sqr.next()
                bb.act(sq[:], kr[:], AF.Square, [kkr], [ksq])
                for s in range(4):
                    bb.mm(p2[:, 16 + s:17 + s], sq[0:64, s * 128:(s + 1) * 128], ones[0:64, 0:1], [ksq], [kp2])
                sm, ksm = smr.next()
                bb.copy('act', sm[:, 16:20], p2[:, 16:20], [kp2], [ksm])
                bb.tt('dve', sm[:, 0:16].rearrange('p (s h) -> p s h', s=4), p2[:, 0:16].rearrange('p (s h) -> p s h', s=4),
                      sm[:, 16:20].unsqueeze(2).to_broadcast([128, 4, 4]), ALU.add, [kp2, ksm], [ksm])
                bb.ts('dve', sm[:, 0:16], sm[:, 0:16], 1.0 / 192, ALU.mult, [ksm], [ksm], s2=EPS, op1=ALU.add)
                bb.act(sm[:, 0:16], sm[:, 0:16], AF.Sqrt, [ksm], [ksm])
                bb.recip(rkm[:, qb * 4:(qb + 1) * 4, :], sm[:, 0:16].rearrange('p (s h) -> p s h', s=4), [ksm], [])
                kg_, kkg = fr.next()
                bb.ts('dve', kg_[:], kr[:], kgr, ALU.mult, [kkr, 'cv'], [kkg])
                kb_, kkb = br.next()
                bb.copy('act', kb_[:], kg_[:], [kkg], [kkb])
                pr_, kpr = pa.next()
                bb.mm(pr_[:], rot, kb_[:], [kkb], [kpr])
                t1, kt1 = fr.next()
                bb.tt('dve', t1[:], pr_[:], sinT[:, cs], ALU.mult, [kpr], [kt1])
                bb.tt('pool', kg_[:], kg_[:], cosT[:, cs], ALU.mult, [kkg], [kkg])
                bb.tt('pool', krot[:, cs], kg_[:], t1[:], ALU.add, [kkg, kt1], [])
                for s in range(4):
                    ps, kp = pa.next()
                    for c in range(4):
                        bb.mm(ps[:], ck[:, c, s * 128:(s + 1) * 128], wupv[:, c, :, :].rearrange('p h n -> p (h n)'), ['wupv', kck], [kp],
                              start=(c == 0), stop=(c == 3))
                    bb.copy('act', vsb[:, qb * 4 + s, :], ps[:], [kp], [])
        with P.phase():
            qnr = Ring([(P.sbuf('ml_qn%d' % i, [128, 512], BF16), 'ml_qn%d' % i) for i in range(2)])
            qrr = Ring([(P.sbuf('ml_qr%d' % i, [128, 512], BF16), 'ml_qr%d' % i) for i in range(2)])
            qrot = Ring([(P.sbuf('ml_qrot%d' % i, [128, 512], BF16), 'ml_qrot%d' % i) for i in range(2)])
            gr = Ring([(P.sbuf('ml_g%d' % i, [128, 512], BF16), 'ml_g%d' % i) for i in range(2)])
            pr = Ring([(P.sbuf('ml_p%d' % i, [128, 512], BF16), 'ml_p%d' % i) for i in range(4)])
            er = Ring([(P.sbuf('ml_e%d' % i, [128, 512], F32), 'ml_e%d' % i) for i in range(3)])
            outr = Ring([(P.sbuf('ml_o%d' % i, [128, 512], BF16), 'ml_o%d' % i) for i in range(2)])
            sring = Ring([(P.psum('ml_st%d' % i, [128, 512], F32), 'ml_st%d' % i) for i in range(4)])
            por = Ring([(P.psum('ml_po%d' % i, [128, 512], F32), 'ml_po%d' % i) for i in range(2)])
            psr = Ring([(P.psum('ml_ps%d' % i, [128, 512], F32), 'ml_ps%d' % i) for i in range(2)])
            for qb in range(NQ):
                cs = slice(qb * 512, (qb + 1) * 512)
                for pair in range(2):
                    qr_, kqr = qrr.next()
                    bb.dma('sp', qr_[:], mlqr_d[pair, :, cs], [], [kqr])
                    prt, kprt = sring.next()
                    bb.mm(prt[:], rot, qr_[:], [kqr], [kprt])
                    e1, k1 = er.next()
                    e2, k2 = er.next()
                    bb.tt('dve', e1[:], prt[:], sinT[:, cs], ALU.mult, [kprt], [k1])
                    bb.tt('pool', e2[:], qr_[:], cosT[:, cs], ALU.mult, [kqr], [k2])
                    qro, kqro = qrot.next()
                    bb.tt('pool', qro[:], e1[:], e2[:], ALU.add, [k1, k2], [kqro])
                    for j in range(2):
                        h = pair * 2 + j
                        rows = slice(64 * j, 64 * j + 64)
                        qn, kqn = qnr.next()
                        g, kg = gr.next()
                        bb.dma('sp', qn[:], mlqn_d[h, :, cs], [], [kqn])
                        bb.dma('sp', g[:], gate_rows(h)[:, cs], [], [kg])
                        po, kpo = por.next()
                        ps, kps = psr.next()
                        for kt in range(NT):
                            st, kst = sring.next()
                            bb.mm(st[:], kn[:, h, kt * 128:(kt + 1) * 128], qn[:], [kqn], [kst], start=True, stop=False)
                            bb.mm(st[:], krot[rows, kt * 128:(kt + 1) * 128], qro[rows, :], [kqro], [kst], start=False, stop=True)
                            pT, kpt = pr.next()
                            bb.act(pT[:], st[:], AF.Exp, [kst], [kpt], scale=rkm[:, kt, h:h + 1])
                            bb.mm(po[:], vsb[:, kt, h * 128:(h + 1) * 128], pT[:], [kpt], [kpo], start=(kt == 0), stop=(kt == NT - 1))
                            bb.mm(ps[:], ones, pT[:], [kpt], [kps], start=(kt == 0), stop=(kt == NT - 1))
                        e, ke = er.next()
                        bb.recip(e[:], ps[:], [kps], [ke])
                        bb.tt('dve', e[:], po[:], e[:], ALU.mult, [kpo, ke], [ke])
                        o, ko = outr.next()
                        bb.tt('pool', o[:], e[:], g[:], ALU.mult, [ke, kg], [ko])
                        bb.dma('pool', y_rows(h)[:, cs], o[:], [ko], [], final=True)


def wout_layout(w):
    w = np.asarray(w, np.float32)
    return np.ascontiguousarray(w.reshape(20, 128, D).transpose(1, 0, 2))


def assemble_y(parts, T):
    y = np.empty((2560, T), parts[0].dtype)
    for hh in range(2):
        p = parts[hh]
        y[hh * 512:(hh + 1) * 512] = p[0:512]
        y[1024 + hh * 512:1024 + (hh + 1) * 512] = p[512:1024]
        y[2048 + hh * 256:2048 + (hh + 1) * 256] = p[1024:1280]
    return y


def prep_B(inp, core, T):
    b, hh = core // 2, core % 2
    w = np.asarray(inp['od_w_in'][0], np.float32)
    O = OD_OFF
    r128 = np.arange(128)
    cols = []
    for h in range(4):
        cols += list(O['sq'] + (4 * hh + h) * 128 + r128)
    cols += list(O['sk'] + hh * 128 + r128)
    for h in range(4):
        cols += list(O['sg'] + (4 * hh + h) * 128 + r128)
    cols += list(O['sv'] + hh * 128 + r128)
    for pair in range(2):
        hA = 4 * hh + 2 * pair
        for H in (hA, hA + 1):
            cols += list(O['mlq'] + H * 192 + r128)
        for H in (hA, hA + 1):
            cols += list(O['mlq'] + H * 192 + 128 + np.arange(64))
    cols += list(O['ckv'] + np.arange(512))
    cols += list(O['kr'] + np.arange(64)) * 2
    for h in range(4):
        cols += list(O['mlg'] + (4 * hh + h) * 128 + r128)
    for h in range(2):
        cols += list(O['mq'] + (2 * hh + h) * 128 + r128)
    for h in range(2):
        cols += list(O['mg'] + (2 * hh + h) * 128 + r128)
    wfm = fm_weight(w, cols).reshape(-1, 128, NCH * 128)
    cv = CV()
    cv.add('gnorm', colvec(inp['od_norm'][0]))
    cv.add('gmem', colvec(inp['od_mem_norm'][0]))
    cv.add('sqg', np.asarray(inp['od_swa_qnorm'][0])[:, None])
    cv.add('skg', np.asarray(inp['od_swa_knorm'][0])[:, None])
    cv.add('sink', bcast(np.asarray(inp['od_swa_sink'][0])[4 * hh:4 * hh + 4]))
    mqn = np.asarray(inp['od_mla_qnorm'][0], np.float32)
    mkn = np.asarray(inp['od_mla_knorm'][0], np.float32)
    cv.add('mlqn', mqn[:128, None])
    cv.add('mlkn', mkn[:128, None])
    cv.add('gqr', np.tile(mqn[128:], 2)[:, None])
    cv.add('kgr', np.tile(mkn[128:], 2)[:, None])
    cv.add('kvg', colvec(inp['od_mla_kv_norm'][0]))
    invf = (np.float32(10000.0) ** (-np.arange(32, dtype=np.float32) / np.float32(32))).astype(np.float32)
    cv.add('invf', np.tile(invf, 4)[:, None])
    cv.add('mqg', np.asarray(inp['od_mem_qnorm'][0])[:, None])
    cv.add('mkg', np.asarray(inp['od_mem_knorm'][0])[:, None])
    rb_ = np.asarray(inp['rel_bias'], np.float32)
    p = np.arange(128)[:, None]
    c = np.arange(128)[None, :]
    toep = np.empty((128, 3, 4, 128), np.float32)
    for oi, off in enumerate((-1, 0, 1)):
        rel = p + off * 128 - c
        bidx = _t5_bucket_np(rel)
        for h in range(4):
            toep[:, oi, h, :] = np.where(np.abs(rel) <= 128, rb_[bidx, 4 * hh + h], NEG)
    wup = np.asarray(inp['od_mla_w_kv_up'][0], np.float32)
    ucols = []
    for h in range(4):
        ucols += list((4 * hh + h) * 256 + r128)
    for h in range(4):
        ucols += list((4 * hh + h) * 256 + 128 + r128)
    wupl = fm_weight(wup, ucols).reshape(8, 128, 512)
    wkv = np.asarray(inp['od_mem_w_kv'][0], np.float32)
    mcols = list(range(2 * hh * 128, (2 * hh + 2) * 128)) + list(range(512 + 2 * hh * 128, 512 + (2 * hh + 2) * 128))
    wmem = fm_weight(wkv, mcols).reshape(-1, 128, NCH * 128)
    pos = np.ascontiguousarray(np.broadcast_to(np.asarray(inp['positions'][b, :T], np.int32)[None, :], (128, T)))
    return dict(x=np.ascontiguousarray(inp['x'][b, :T]), wo=wout_layout(inp['ev_w_out'][0]), wfm=wfm, cvec=cv.arr(), cmat=CM_ARR,
                toep=toep, mem=np.asarray(inp['mem'][b], np.float32), wmem=wmem, wup=wupl, pos=pos), cv.off


def build_B(T, cvoff, ncv, phases=('wout', 'proj', 'swa', 'mla', 'mem')):
    nc = bass.Bass("TRN2", target_bir_lowering=False)
    NT = T // 128
    x_d = nc.dram_tensor("x", [T, D], F32, kind="ExternalInput").ap()
    y0_d = nc.dram_tensor("y0T", [2560, T], BF16, kind="ExternalInput").ap()
    wo_d = nc.dram_tensor("wo", [128, 20, D], F32, kind="ExternalInput").ap()
    wfm_d = nc.dram_tensor("wfm", [29, 128, D], F32, kind="ExternalInput").ap()
    cvec_d = nc.dram_tensor("cvec", [128, ncv], F32, kind="ExternalInput").ap()
    cmat_d = nc.dram_tensor("cmat", [128, len(CM_NAMES), 128], F32, kind="ExternalInput").ap()
    toep_d = nc.dram_tensor("toep", [128, 3, 4, 128], F32, kind="ExternalInput").ap()
    mem_d = nc.dram_tensor("mem", [256, D], F32, kind="ExternalInput").ap()
    wmem_d = nc.dram_tensor("wmem", [4, 128, D], F32, kind="ExternalInput").ap()
    wup_d = nc.dram_tensor("wup", [8, 128, 512], F32, kind="ExternalInput").ap()
    pos_d = nc.dram_tensor("pos", [128, T], I32, kind="ExternalInput").ap()
    x1_d = nc.dram_tensor("x1", [T, D], F32, kind="ExternalOutput").ap()
    y_d = nc.dram_tensor("yT", [1280, T], BF16, kind="ExternalOutput").ap()
    sq_d = nc.dram_tensor("sqT", [4, 128, T], BF16, kind="Internal").ap()
    sk_d = nc.dram_tensor("skT", [128, T], BF16, kind="Internal").ap()
    sv_d = nc.dram_tensor("sv_tok", [T, 128], BF16, kind="Internal").ap()
    gate_d = nc.dram_tensor("gateT", [10, 128, T], BF16, kind="Internal").ap()
    mlqn_d = nc.dram_tensor("mlqn", [4, 128, T], BF16, kind="Internal").ap()
    mlqr_d = nc.dram_tensor("mlqr", [2, 128, T], BF16, kind="Internal").ap()
    ckv_d = nc.dram_tensor("ckvT", [4, 128, T], BF16, kind="Internal").ap()
    kr_d = nc.dram_tensor("krT", [128, T], F32, kind="Internal").ap()
    mq_d = nc.dram_tensor("mqT", [2, 128, T], BF16, kind="Internal").ap()
    P = Prog(nc)
    bb = B(P)
    cm, cmb, cv = load_consts(P, bb, cmat_d, cvec_d, ncv)

    def cvc(name, j=0, n=1):
        o, w = cvoff[name]
        return cv[:, o + j:o + j + n]
    rk_s = P.sbuf('rk_s', [128, NT, 1], F32, persistent=True)
    comb = P.sbuf('comb', [128, 4], F32, persistent=True)
    bb.stt('dve', comb[:, 0:1], cvc('sqg'), 128 ** -0.5, cvc('skg'), ALU.mult, ALU.mult, ['cv'], ['comb'])
    bb.stt('dve', comb[:, 1:2], cvc('mqg'), 128 ** -0.5, cvc('mkg'), ALU.mult, ALU.mult, ['cv'], ['comb'])
    bb.stt('dve', comb[:, 2:3], cvc('mlqn'), 192 ** -0.5, cvc('mlkn'), ALU.mult, ALU.mult, ['cv'], ['comb'])
    bb.ts('dve', comb[:, 3:4], cvc('gqr'), 192 ** -0.5, ALU.mult, ['cv'], ['comb'])
    P.emit()
    if 'wout' in phases:
        phase_wout(P, bb, 0, T, y0_d, wo_d, lambda tt: x_d[tt * 128:(tt + 1) * 128, :], lambda tt: x1_d[tt * 128:(tt + 1) * 128, :], True)
    if 'proj' in phases:
        with ExitStack() as st:
            xnT = st.enter_context(nc.sbuf_tensor('xnT', [128, NCH, T], BF16))
            src = x1_d if 'wout' in phases else x_d
            phase_xnT(P, bb, T, lambda tt: src[tt * 128:(tt + 1) * 128, :], xnT, cmb('ident'))
            gcol = cvc('gnorm', 0, NCH)
            blk = lambda qb: slice(qb * 512, (qb + 1) * 512)
            specs = []
            for h in range(4):
                specs.append(dict(kind='qnorm', gs=128, comb=comb[:, 0:1], dst=lambda qb, h=h: sq_d[h, :, blk(qb)]))
            specs.append(dict(kind='kraw', gs=128, dst=lambda qb: sk_d[:, blk(qb)], rk=lambda qb: rk_s[:, qb * 4:(qb + 1) * 4, 0:1]))
            for h in range(4):
                specs.append(dict(kind='silu', dst=lambda qb, h=h: gate_d[h, :, blk(qb)]))
            specs.append(dict(kind='tm', dt='bf16', ncols=128, dst=lambda qb: sv_d[blk(qb), :].rearrange('(s p) n -> p s n', p=128)))
            for pair in range(2):
                specs.append(dict(kind='hold'))
                specs.append(dict(kind='hold'))
                specs.append(dict(kind='mlaq', combn=comb[:, 2:3], gqr=comb[:, 3:4],
                                  dstn=lambda j, qb, pair=pair: mlqn_d[2 * pair + j, :, blk(qb)],
                                  dstr=lambda qb, pair=pair: mlqr_d[pair, :, blk(qb)]))
            specs += [dict(kind='hold')] * 3
            specs.append(dict(kind='ckv', gains=cvc('kvg', 0, 4), dst=lambda j, qb: ckv_d[j, :, blk(qb)]))
            specs.append(dict(kind='raw32', dst=lambda qb: kr_d[:, blk(qb)]))
            for h in range(4):
                specs.append(dict(kind='silu', dst=lambda qb, h=h: gate_d[4 + h, :, blk(qb)]))
            for h in range(2):
                specs.append(dict(kind='qnorm', gs=128, comb=comb[:, 1:2], dst=lambda qb, h=h: mq_d[h, :, blk(qb)]))
            for h in range(2):
                specs.append(dict(kind='silu', dst=lambda qb, h=h: gate_d[8 + h, :, blk(qb)]))
            phase_proj(P, bb, T, xnT, gcol, wfm_d, specs, cmb)
    if 'swa' in phases:
        phase_swa(P, bb, T, sq_d, sk_d, sv_d, rk_s, toep_d, lambda h: gate_d[h], lambda h: y_d[h * 128:(h + 1) * 128, :], cmb,
                  cvc('sink', 0, 4))
    if 'mla' in phases:
        phase_mla(P, bb, T, mlqn_d, mlqr_d, ckv_d, kr_d, wup_d, pos_d, lambda h: gate_d[4 + h],
                  lambda h: y_d[(4 + h) * 128:(5 + h) * 128, :], cm, cmb, cvc('invf'), cvc('kgr'))
    if 'mem' in phases:
        phase_mem(P, bb, T, mem_d, wmem_d, cvc('gmem', 0, NCH), mq_d, lambda h: gate_d[8 + h],
                  lambda h: y_d[(8 + h) * 128:(9 + h) * 128, :], cmb)
    P.finish()
    return nc


def build_C(TH):
    nc = bass.Bass("TRN2", target_bir_lowering=False)
    x_d = nc.dram_tensor("x1", [TH, D], F32, kind="ExternalInput").ap()
    y_d = nc.dram_tensor("y1T", [2560, TH], BF16, kind="ExternalInput").ap()
    wo_d = nc.dram_tensor("wo", [128, 20, D], F32, kind="ExternalInput").ap()
    out_d = nc.dram_tensor("out", [TH, D], F32, kind="ExternalOutput").ap()
    P = Prog(nc)
    bb = B(P)
    phase_wout(P, bb, 0, TH, y_d, wo_d, lambda tt: x_d[tt * 128:(tt + 1) * 128, :], lambda tt: out_d[tt * 128:(tt + 1) * 128, :], True)
    P.finish()
    return nc


def run_all(inputs, T):
    n = 8
    cores = list(range(n))
    inp = inputs
    mapsA, offA = [], None
    for c in cores:
        m, offA = prep_A({**inp, 'x': np.asarray(inp['x'])[:, :T]}, c)
        mapsA.append(m)
    ncA = build_A(T, offA, mapsA[0]['cvec'].shape[1])
    resA = run_bass_kernel_spmd(ncA, mapsA, core_ids=cores).results
    y0 = [assemble_y([np.asarray(resA[2 * b + hh]['yT']) for hh in range(2)], T) for b in range(4)]
    del mapsA
    mapsB, offB = [], None
    for c in cores:
        m, offB = prep_B(inp, c, T)
        m['y0T'] = y0[c // 2]
        mapsB.append(m)
    ncB = build_B(T, offB, mapsB[0]['cvec'].shape[1])
    resB = run_bass_kernel_spmd(ncB, mapsB, core_ids=cores).results
    y1 = [assemble_y([np.asarray(resB[2 * b + hh]['yT']) for hh in range(2)], T) for b in range(4)]
    x1 = [np.asarray(resB[2 * b]['x1']) for b in range(4)]
    del mapsB
    TH = T // 2
    wo1 = wout_layout(inp['od_w_out'][0])
    mapsC = []
    for c in cores:
        b, half = c // 2, c % 2
        mapsC.append(dict(x1=np.ascontiguousarray(x1[b][half * TH:(half + 1) * TH]),
                          y1T=np.ascontiguousarray(y1[b][:, half * TH:(half + 1) * TH]), wo=wo1))
    ncC = build_C(TH)
    resC = run_bass_kernel_spmd(ncC, mapsC, core_ids=cores).results
    out = np.empty((4, T, D), np.float32)
    for c in cores:
        b, half = c // 2, c % 2
        out[b, half * TH:(half + 1) * TH] = np.asarray(resC[c]['out'])
    return out


def kernel(**inputs):
    inputs = {k: np.asarray(v) for k, v in inputs.items()}
    return run_all(inputs, 4096)
```
